# Optimizing a Trainium2 kernel written in Bass

```python
import math
import jax
import jax.numpy as jnp
from jax import lax
import numpy as np


D_MODEL = 2048
BATCH = 8
SEQ = 2048
DEPTH = 1

MEM_LEN = 256
MIX_WIDTH = D_MODEL
MOBA_WIDTH = MIX_WIDTH // 2
MOBA_HEADS = 8
MOBA_HEAD_DIM = MOBA_WIDTH // MOBA_HEADS
MOBA_BLOCK = 256
MOBA_TOPK = 3
MOBA_Q_CHUNK = 16
GLA_WIDTH = MIX_WIDTH - MOBA_WIDTH
GLA_HEADS = 4
GLA_KEY_DIM = GLA_WIDTH // (2 * GLA_HEADS)
GLA_VAL_DIM = GLA_WIDTH // GLA_HEADS
GLA_GATE_RANK = 16
GLA_GATE_TAU = 16.0
GLA_CHUNK = 64
XATTN_HEADS = 4
XATTN_HEAD_DIM = D_MODEL // XATTN_HEADS
PEER_HEADS = 8
PEER_N_KEYS = 128
PEER_N_EXPERTS = PEER_N_KEYS * PEER_N_KEYS
PEER_QUERY_DIM = 256
PEER_TOPK = 16
PEER_TOKEN_CHUNK = 128
DEEPNORM_ALPHA = (2.0 * DEPTH) ** 0.25
DEEPNORM_BETA = (8.0 * DEPTH) ** -0.25
LN_EPS = 1e-5
RMS_EPS = 1e-6
NEG_INF = -1e30
W_IN_SIZES = (MOBA_WIDTH, MOBA_WIDTH, MOBA_WIDTH, GLA_HEADS * GLA_KEY_DIM, GLA_HEADS * GLA_KEY_DIM, GLA_WIDTH, GLA_WIDTH, GLA_GATE_RANK)
W_IN_COLS = 3 * MOBA_WIDTH + 2 * GLA_HEADS * GLA_KEY_DIM + 2 * GLA_WIDTH + GLA_GATE_RANK

kernel_name = 'hybrid_moba_gla_peer_deepnorm'


def _split_heads(a, n_heads):
    b, s, _ = a.shape
    return a.reshape(b, s, n_heads, -1).transpose(0, 2, 1, 3)


def _merge_heads(a):
    b, h, s, d = a.shape
    return a.transpose(0, 2, 1, 3).reshape(b, s, h * d)


def layer_norm(x, g, b):
    xf = x.astype(jnp.float32)
    mu = jnp.mean(xf, axis=-1, keepdims=True)
    var = jnp.mean(jnp.square(xf - mu), axis=-1, keepdims=True)
    y = (xf - mu) * lax.rsqrt(var + LN_EPS) * g.astype(jnp.float32) + b.astype(jnp.float32)
    return y.astype(x.dtype)


def moba_attention(q, k, v):
    B, H, S, Dh = q.shape
    nb = -(-S // MOBA_BLOCK)
    s_pad = nb * MOBA_BLOCK
    pad = ((0, 0), (0, 0), (0, s_pad - S), (0, 0))
    kb = jnp.pad(k, pad).reshape(B, H, nb, MOBA_BLOCK, Dh)
    vb = jnp.pad(v, pad).reshape(B, H, nb, MOBA_BLOCK, Dh)
    k_mean = jnp.mean(kb.astype(jnp.float32), axis=3)
    gate = jnp.einsum('bhsd,bhnd->bhsn', q.astype(jnp.float32), k_mean)
    q_block = jnp.arange(S) // MOBA_BLOCK
    past = jnp.arange(nb)[None, :] < q_block[:, None]
    gate = jnp.where(past, gate, NEG_INF)
    n_sel = min(MOBA_TOPK, nb)
    sel_score, sel_idx = lax.top_k(gate, n_sel)
    sel_valid = sel_score > 0.5 * NEG_INF

    nc = S // MOBA_Q_CHUNK

    def to_chunks(a):
        a = a.reshape(B, H, nc, MOBA_Q_CHUNK, *a.shape[3:])
        return jnp.moveaxis(a, 2, 0)

    starts = jnp.arange(nc) * MOBA_Q_CHUNK
    b_idx = jnp.arange(B)[:, None, None, None]
    h_idx = jnp.arange(H)[None, :, None, None]
    scale = Dh ** -0.5
    key_off = jnp.arange(MOBA_BLOCK)
    q_off = jnp.arange(MOBA_Q_CHUNK)

    def attend(args):
        qc, idx, valid, start = args
        k_sel = kb[b_idx, h_idx, idx]
        v_sel = vb[b_idx, h_idx, idx]
        own = start // MOBA_BLOCK
        k_own = lax.dynamic_index_in_dim(kb, own, axis=2, keepdims=False)
        v_own = lax.dynamic_index_in_dim(vb, own, axis=2, keepdims=False)
        s_sel = jnp.einsum('bhqd,bhqtkd->bhqtk', qc, k_sel).astype(jnp.float32) * scale
        s_sel = jnp.where(valid[..., None], s_sel, NEG_INF)
        s_own = jnp.einsum('bhqd,bhkd->bhqk', qc, k_own).astype(jnp.float32) * scale
        causal = (own * MOBA_BLOCK + key_off)[None, :] <= (start + q_off)[:, None]
        s_own = jnp.where(causal, s_own, NEG_INF)
        scores = jnp.concatenate([s_sel.reshape(B, H, MOBA_Q_CHUNK, -1), s_own], axis=-1)
        p = jax.nn.softmax(scores, axis=-1).astype(v.dtype)
        p_sel = p[..., : n_sel * MOBA_BLOCK].reshape(B, H, MOBA_Q_CHUNK, n_sel, MOBA_BLOCK)
        p_own = p[..., n_sel * MOBA_BLOCK:]
        return (jnp.einsum('bhqtk,bhqtkd->bhqd', p_sel, v_sel)
                + jnp.einsum('bhqk,bhkd->bhqd', p_own, v_own))

    out = lax.map(attend, (to_chunks(q), to_chunks(sel_idx), to_chunks(sel_valid), starts))
    return jnp.moveaxis(out, 0, 2).reshape(B, H, S, Dh)


def gla_attention(q, k, v, log_a):
    B, H, S, Dk = q.shape
    Dv = v.shape[-1]
    n = S // GLA_CHUNK
    f32 = jnp.float32
    qf = q.astype(f32).reshape(B, H, n, GLA_CHUNK, Dk) * Dk ** -0.5
    kf = k.astype(f32).reshape(B, H, n, GLA_CHUNK, Dk)
    vf = v.astype(f32).reshape(B, H, n, GLA_CHUNK, Dv)
    cum = jnp.cumsum(log_a.astype(f32).reshape(B, H, n, GLA_CHUNK, Dk), axis=3)
    cum_last = cum[..., -1:, :]
    q_dec = qf * jnp.exp(cum)
    k_dec = kf * jnp.exp(-cum)
    k_state = kf * jnp.exp(cum_last - cum)
    causal = jnp.tril(jnp.ones((GLA_CHUNK, GLA_CHUNK), dtype=bool))
    a_intra = jnp.where(causal, jnp.einsum('bhnid,bhnjd->bhnij', q_dec, k_dec), 0.0)
    o_intra = jnp.einsum('bhnij,bhnje->bhnie', a_intra, vf)
    delta = jnp.einsum('bhncd,bhnce->bhnde', k_state, vf)
    decay = jnp.exp(cum_last[..., 0, :])

    def step(state, inp):
        dec, dlt = inp
        return state * dec[..., None] + dlt, state

    _, states = lax.scan(step, jnp.zeros((B, H, Dk, Dv), f32),
                         (jnp.moveaxis(decay, 2, 0), jnp.moveaxis(delta, 2, 0)))
    states = jnp.moveaxis(states, 0, 2)
    o_inter = jnp.einsum('bhncd,bhnde->bhnce', q_dec, states)
    return (o_intra + o_inter).reshape(B, H, S, Dv)


def hybrid_mixer(x, w_in, gla_gate_up, gla_gate_bias, gla_norm_g, w_out):
    proj = x @ w_in
    cuts = np.cumsum(W_IN_SIZES)[:-1].tolist()
    q_m, k_m, v_m, q_g, k_g, v_g, r_g, gate_lr = jnp.split(proj, cuts, axis=-1)
    o_moba = moba_attention(_split_heads(q_m, MOBA_HEADS), _split_heads(k_m, MOBA_HEADS),
                            _split_heads(v_m, MOBA_HEADS))
    o_moba = _merge_heads(o_moba)
    log_a = jax.nn.log_sigmoid((gate_lr @ gla_gate_up + gla_gate_bias).astype(jnp.float32)) / GLA_GATE_TAU
    o_gla = gla_attention(_split_heads(q_g, GLA_HEADS), _split_heads(k_g, GLA_HEADS),
                          _split_heads(v_g, GLA_HEADS), _split_heads(log_a, GLA_HEADS))
    o_gla = o_gla * lax.rsqrt(jnp.mean(jnp.square(o_gla), axis=-1, keepdims=True) + RMS_EPS)
    o_gla = o_gla * gla_norm_g.astype(jnp.float32)
    o_gla = (_merge_heads(o_gla) * jax.nn.silu(r_g.astype(jnp.float32))).astype(x.dtype)
    return jnp.concatenate([o_moba, o_gla], axis=-1) @ w_out


def memory_cross_attention(x, mem, w_q, w_kv, w_o):
    q = _split_heads(x @ w_q, XATTN_HEADS)
    k, v = jnp.split(mem @ w_kv, 2, axis=-1)
    k = _split_heads(k, XATTN_HEADS)
    v = _split_heads(v, XATTN_HEADS)
    s = jnp.einsum('bhsd,bhmd->bhsm', q, k).astype(jnp.float32) * XATTN_HEAD_DIM ** -0.5
    p = jax.nn.softmax(s, axis=-1).astype(v.dtype)
    o = jnp.einsum('bhsm,bhmd->bhsd', p, v)
    return _merge_heads(o) @ w_o


def peer_ffn(x, w_query, sub_keys, expert_down, expert_up):
    B, S, D = x.shape
    T = B * S
    xt = x.reshape(T, D)
    q = (xt @ w_query).reshape(T, PEER_HEADS, 2, PEER_QUERY_DIM // 2)
    s = jnp.einsum('thpd,hpnd->thpn', q, sub_keys).astype(jnp.float32)
    s_top, i_top = lax.top_k(s, PEER_TOPK)
    cand = s_top[:, :, 0, :, None] + s_top[:, :, 1, None, :]
    cand_idx = i_top[:, :, 0, :, None] * PEER_N_KEYS + i_top[:, :, 1, None, :]
    cand = cand.reshape(T, PEER_HEADS, PEER_TOPK * PEER_TOPK)
    cand_idx = cand_idx.reshape(T, PEER_HEADS, PEER_TOPK * PEER_TOPK)
    best, pos = lax.top_k(cand, PEER_TOPK)
    expert_idx = jnp.take_along_axis(cand_idx, pos, axis=-1)
    gates = jax.nn.softmax(best, axis=-1)
    nc = T // PEER_TOKEN_CHUNK

    def apply(args):
        xc, idx, g = args
        u = expert_down[idx]
        h = jax.nn.gelu(jnp.einsum('td,thkd->thk', xc, u).astype(jnp.float32), approximate=False)
        vsel = expert_up[idx]
        return jnp.einsum('thk,thkd->td', (g * h).astype(x.dtype), vsel)

    out = lax.map(apply, (xt.reshape(nc, PEER_TOKEN_CHUNK, D),
                          expert_idx.reshape(nc, PEER_TOKEN_CHUNK, PEER_HEADS, PEER_TOPK),
                          gates.reshape(nc, PEER_TOKEN_CHUNK, PEER_HEADS, PEER_TOPK)))
    return out.reshape(B, S, D)


def setup_inputs(seed: int = 0) -> dict:
    key = jax.random.key(seed)
    ks = jax.random.split(key, 20)
    L = DEPTH
    d = D_MODEL

    def nrm(k, shape, scale):
        return jax.random.normal(k, shape, jnp.float32) * scale

    return {
        'x': nrm(ks[0], (BATCH, SEQ, d), 1.0),
        'mem': nrm(ks[1], (BATCH, MEM_LEN, d), 1.0),
        'w_in': nrm(ks[2], (L, d, W_IN_COLS), d ** -0.5),
        'gla_gate_up': nrm(ks[3], (L, GLA_GATE_RANK, GLA_HEADS * GLA_KEY_DIM), GLA_GATE_RANK ** -0.5),
        'gla_gate_bias': nrm(ks[4], (L, GLA_HEADS * GLA_KEY_DIM), 0.1),
        'gla_norm_g': 1.0 + nrm(ks[5], (L, GLA_VAL_DIM), 0.02),
        'w_out': nrm(ks[6], (L, MIX_WIDTH, d), DEEPNORM_BETA * MIX_WIDTH ** -0.5),
        'ln1_g': 1.0 + nrm(ks[7], (L, d), 0.02),
        'ln1_b': nrm(ks[8], (L, d), 0.02),
        'xattn_wq': nrm(ks[9], (L, d, d), d ** -0.5),
        'xattn_wkv': nrm(ks[10], (L, d, 2 * d), d ** -0.5),
        'xattn_wo': nrm(ks[11], (L, d, d), DEEPNORM_BETA * d ** -0.5),
        'ln2_g': 1.0 + nrm(ks[12], (L, d), 0.02),
        'ln2_b': nrm(ks[13], (L, d), 0.02),
        'peer_wq': nrm(ks[14], (L, d, PEER_HEADS * PEER_QUERY_DIM), d ** -0.5),
        'peer_sub_keys': nrm(ks[15], (L, PEER_HEADS, 2, PEER_N_KEYS, PEER_QUERY_DIM // 2), (PEER_QUERY_DIM // 2) ** -0.5),
        'peer_u': nrm(ks[16], (L, PEER_N_EXPERTS, d), d ** -0.5),
        'peer_v': nrm(ks[17], (L, PEER_N_EXPERTS, d), DEEPNORM_BETA * PEER_HEADS ** -0.5),
        'ln3_g': 1.0 + nrm(ks[18], (L, d), 0.02),
        'ln3_b': nrm(ks[19], (L, d), 0.02),
    }


def reference(x, mem, w_in, gla_gate_up, gla_gate_bias, gla_norm_g, w_out, ln1_g, ln1_b,
              xattn_wq, xattn_wkv, xattn_wo, ln2_g, ln2_b,
              peer_wq, peer_sub_keys, peer_u, peer_v, ln3_g, ln3_b):
    h = x
    for l in range(DEPTH):
        mix = hybrid_mixer(h, w_in[l], gla_gate_up[l], gla_gate_bias[l], gla_norm_g[l], w_out[l])
        h = layer_norm(DEEPNORM_ALPHA * h + mix, ln1_g[l], ln1_b[l])
        xa = memory_cross_attention(h, mem, xattn_wq[l], xattn_wkv[l], xattn_wo[l])
        h = layer_norm(DEEPNORM_ALPHA * h + xa, ln2_g[l], ln2_b[l])
        ff = peer_ffn(h, peer_wq[l], peer_sub_keys[l], peer_u[l], peer_v[l])
        h = layer_norm(DEEPNORM_ALPHA * h + ff, ln3_g[l], ln3_b[l])
    return h
```

```python
import numpy as np
import concourse.bass as bass
import concourse.mybir as mybir
from concourse.bass_utils import run_bass_kernel_spmd

F32 = mybir.dt.float32
BF16 = mybir.dt.bfloat16
U32 = mybir.dt.uint32
AF = mybir.ActivationFunctionType
ALU = mybir.AluOpType
AX = mybir.AxisListType

S = 2048
D = 2048
NT = 16
ALPHA = 2.0 ** 0.25
NEG = -1.0e30
ENGS = ("sp", "pe", "act", "dve", "pool")
DBG = None


class Prog:
    def __init__(self, nc):
        self.nc = nc
        self.ops = {e: [] for e in ENGS}
        self.cnt = {e: 0 for e in ENGS}
        self.lastw = {}
        self.readers = {}
        self.waited = {e: {} for e in ENGS}
        self.dpool = {"sp": ["ds%d" % i for i in range(20)], "pool": ["dp%d" % i for i in range(12)],
                      "act": ["da%d" % i for i in range(8)]}
        self.drr = {"sp": 0, "pool": 0, "act": 0}
        self.dtot = {}
        for q in self.dpool:
            for n in self.dpool[q]:
                self.dtot[n] = 0

    def _need(self, eng, reads, writes):
        need = {}

        def add(sv):
            if sv is None:
                return
            s, v = sv
            if need.get(s, 0) < v:
                need[s] = v

        for k in reads:
            add(self.lastw.get(k))
        for k in writes:
            add(self.lastw.get(k))
            for s, v in self.readers.get(k, {}).items():
                add((s, v))
        for s, v in need.items():
            if eng == "pe" and s == "pe":
                continue
            if self.waited[eng].get(s, 0) < v:
                self.waited[eng][s] = v
                self.ops[eng].append(("w", s, v))

    def _mark(self, sem, val, reads, writes):
        for k in reads:
            self.readers.setdefault(k, {})[sem] = val
        for k in writes:
            self.lastw[k] = (sem, val)
            self.readers[k] = {}

    capture = None

    def op(self, eng, meth, reads=(), writes=(), **kw):
        if self.capture is not None:
            self.capture.append((eng, meth, list(reads), list(writes), kw))
            return
        self._need(eng, reads, writes)
        self.cnt[eng] += 1
        self.ops[eng].append(("i", meth, kw))
        self._mark(eng, self.cnt[eng], reads, writes)

    def dma(self, q, out, in_, reads=(), writes=(), **kw):
        self._need(q, reads, writes)
        pool = self.dpool[q]
        name = pool[self.drr[q] % len(pool)]
        self.drr[q] += 1
        prev = self.dtot[name]
        if prev and self.waited[q].get(name, 0) < prev:
            self.waited[q][name] = prev
            self.ops[q].append(("w", name, prev))
        self.dtot[name] = prev + 16
        self.ops[q].append(("d", lambda e: e.dma_start(out=out, in_=in_, **kw), name))
        self._mark(name, prev + 16, reads, writes)
        if q == "pool":
            self.waited[q][name] = prev + 16
            self.ops[q].append(("w", name, prev + 16))

    def replay(self, A, B=()):
        A, B = list(A), list(B)
        step = max(1, len(A) // (len(B) + 1)) if B else 0
        bi = 0
        for i, o in enumerate(A):
            self.op(o[0], o[1], o[2], o[3], **o[4])
            if B and (i + 1) % step == 0 and bi < len(B):
                o2 = B[bi]
                bi += 1
                self.op(o2[0], o2[1], o2[2], o2[3], **o2[4])
        for o2 in B[bi:]:
            self.op(o2[0], o2[1], o2[2], o2[3], **o2[4])

    def barrier(self):
        for e in ENGS:
            for o in ENGS:
                if self.cnt[o] and self.waited[e].get(o, 0) < self.cnt[o]:
                    self.waited[e][o] = self.cnt[o]
                    self.ops[e].append(("w", o, self.cnt[o]))
            for n, v in self.dtot.items():
                if v and self.waited[e].get(n, 0) < v:
                    self.waited[e][n] = v
                    self.ops[e].append(("w", n, v))

    def emit(self):
        nc = self.nc
        sems = {}
        for e in ENGS:
            sems[e] = nc.alloc_semaphore("sem_" + e)
        for n in self.dtot:
            sems[n] = nc.alloc_semaphore("sem_" + n)
        self.barrier()

        def run(eng, key):
            for o in self.ops[key]:
                if o[0] == "w":
                    eng.wait_ge(sems[o[1]], o[2])
                elif o[0] == "i":
                    getattr(eng, o[1])(**o[2]).then_inc(sems[key], 1)
                else:
                    o[1](eng).then_inc(sems[o[2]], 16)

        with nc.allow_low_precision("small integer indices / 0-1 masks are exact in bf16"), nc.Block() as block:
            @block.sync
            def _(e):
                run(e, "sp")

            @block.tensor
            def _(e):
                run(e, "pe")

            @block.scalar
            def _(e):
                run(e, "act")

            @block.vector
            def _(e):
                run(e, "dve")

            @block.gpsimd
            def _(e):
                run(e, "pool")


class _Stop(Exception):
    pass


def build_program():
    nc = bass.Bass("TRN2", target_bir_lowering=False)
    try:
        _build(nc)
    except _Stop:
        pass
    return nc


def _build(nc):
    P = Prog(nc)

    def ckpt(name):
        if DBG == name:
            P.barrier()
            for tt_ in range(16):
                P.dma("sp", out_d[tt_ * 128:(tt_ + 1) * 128, :], xtok_d[tt_ * 128:(tt_ + 1) * 128, :])
            P.emit()
            raise _Stop()

    def din(name, shape, dt=F32):
        return nc.dram_tensor(name, shape, dt, kind="ExternalInput").ap()

    def dscr(name, shape, dt):
        return nc.dram_tensor(name, shape, dt, kind="Internal").ap()

    xT_d = din("xT", [2048, 2048])
    xtok_d = din("xtok", [2048, 2048])
    memT_d = din("memT", [2048, 256])
    win_d = din("w_in", [128, 48 * 2048])
    wgl_d = din("w_gl", [128, 256])
    gup_d = din("gate_up", [16, 512])
    gbias_d = din("gate_bias", [1, 512])
    gnorm_d = din("gnorm", [128, 256])
    wout_d = din("w_out", [128, 16 * 2048])
    wq_d = din("wq", [128, 16 * 2048])
    wkv_d = din("wkv", [128, 16 * 16 * 256])
    wo_d = din("wo", [128, 16 * 2048])
    pwq_d = din("pwq", [128, 16 * 2048])
    skT_d = din("skT", [128, 16 * 128])
    NUV = 128 * 128 if DBG is None else 128
    UT_d = din("UT", [NUV, 2048])
    V_d = din("Vp", [NUV, 2048])
    lnp_d = din("lnp", [6, 128, 2048])
    c_ident = din("c_ident", [128, 128])
    c_triL = din("c_triL", [128, 128])
    c_triU = din("c_triU", [128, 128])
    c_iota = din("c_iota", [128, 128])
    out_d = nc.dram_tensor("out", [2048, 2048], F32, kind="ExternalOutput").ap()

    win_s = dscr("win_s", [128, 48 * 2048], BF16)
    wout_s = dscr("wout_s", [128, 16 * 2048], BF16)
    wq_s = dscr("wq_s", [128, 16 * 2048], BF16)
    wkv_s = dscr("wkv_s", [128, 16 * 16 * 256], BF16)
    wo_s = dscr("wo_s", [128, 16 * 2048], BF16)
    pwq_s = dscr("pwq_s", [128, 16 * 2048], BF16)
    UT_s = dscr("UT_s", [NUV, 2048], BF16)
    V_s = dscr("V_s", [NUV, 2048], BF16)
    h1_s = dscr("h1_s", [2048, 2048], F32)
    h2_s = dscr("h2_s", [2048, 2048], F32)

    BUFA = nc.alloc_sbuf_tensor("BUFA", [128, 32768], BF16)
    BUFB = nc.alloc_sbuf_tensor("BUFB", [128, 32768], BF16)
    identb = nc.alloc_sbuf_tensor("identb", [128, 128], BF16)
    identf = nc.alloc_sbuf_tensor("identf", [128, 128], F32)
    triLb = nc.alloc_sbuf_tensor("triLb", [128, 128], BF16)
    triLf = nc.alloc_sbuf_tensor("triLf", [128, 128], F32)
    triUf = nc.alloc_sbuf_tensor("triUf", [128, 128], F32)
    iotaf = nc.alloc_sbuf_tensor("iotaf", [128, 128], F32)
    onesb = nc.alloc_sbuf_tensor("onesb", [128, 2], BF16)
    onesf = nc.alloc_sbuf_tensor("onesf", [128, 128], F32)
    WORKN = 19200
    WORK = nc.alloc_sbuf_tensor("WORK", [128, WORKN], F32)
    PSALL = nc.alloc_psum_tensor("psall", [128, 4096], F32)

    wk = {"off": 0}

    def carve(shape, dt):
        n = 1
        for s_ in shape[1:]:
            n *= s_
        words = n if dt in (F32, U32) else (n + 1) // 2
        words = (words + 7) // 8 * 8
        o = wk["off"]
        assert o + words <= WORKN, ("WORK overflow", o, words)
        wk["off"] = o + words
        v = WORK[0:shape[0], o:o + words]
        if dt == BF16:
            v = v.bitcast(BF16)[:, 0:n]
        elif dt == U32:
            v = v.bitcast(U32)[:, 0:n]
        else:
            v = v[:, 0:n]
        if len(shape) == 3:
            v = v.rearrange("p (a b) -> p a b", b=shape[2])
        elif len(shape) == 4:
            v = v.rearrange("p (a b c) -> p a b c", b=shape[2], c=shape[3])
        return v

    def phase_reset():
        P.barrier()
        wk["off"] = 0

    A3 = BUFA[:].rearrange("p (c t) -> p c t", t=2048)
    B3 = BUFB[:].rearrange("p (c t) -> p c t", t=2048)

    def psb(i, dt=F32):
        v = PSALL[:, i * 512:(i + 1) * 512]
        return v if dt == F32 else v.bitcast(BF16)

    def cast_w(name, dst, src, rows, cols, parts=1):
        step = rows // parts
        b = min(cols, 2048)
        for i in range(parts):
            s_ = src[i * step:(i + 1) * step, :].rearrange("p (a b) -> p a b", b=b)
            d_ = dst[i * step:(i + 1) * step, :].rearrange("p (a b) -> p a b", b=b)
            P.dma("pool", d_, s_, writes=[(name, i)])
        return [(name, i) for i in range(parts)]

    P.dma("sp", identf[:], c_ident, writes=["identf"])
    P.dma("sp", triLf[:], c_triL, writes=["triLf"])
    P.dma("sp", triUf[:], c_triU, writes=["triUf"])
    P.dma("sp", iotaf[:], c_iota, writes=["iotaf"])
    P.op("dve", "tensor_copy", out=identb[:], in_=identf[:], reads=["identf"], writes=["identb"])
    P.op("dve", "tensor_copy", out=triLb[:], in_=triLf[:], reads=["triLf"], writes=["triLb"])
    P.op("dve", "memset", ap=onesb[:], constant=1.0, writes=["onesb"])
    P.op("dve", "memset", ap=onesf[:], constant=1.0, writes=["onesf"])

    for c in range(16):
        P.dma("pool", A3[:, c, :], xT_d[c * 128:(c + 1) * 128, :], writes=[("xT", c)])
    k_win = cast_w("win_s", win_s, win_d, 128, 48 * 2048, parts=1)
    k_wout = cast_w("wout_s", wout_s, wout_d, 128, 16 * 2048)
    k_wkv = cast_w("wkv_s", wkv_s, wkv_d, 128, 16 * 16 * 256)
    k_wq = cast_w("wq_s", wq_s, wq_d, 128, 16 * 2048)
    k_wo = cast_w("wo_s", wo_s, wo_d, 128, 16 * 2048)
    k_pwq = cast_w("pwq_s", pwq_s, pwq_d, 128, 16 * 2048)
    k_UT = cast_w("UT_s", UT_s, UT_d, NUV, 2048, parts=8)
    k_V = cast_w("V_s", V_s, V_d, NUV, 2048, parts=8)

    if DBG == "p0":
        P.barrier()
        for tt in range(16):
            P.dma("sp", out_d[tt * 128:(tt + 1) * 128, :], xtok_d[tt * 128:(tt + 1) * 128, :])
        P.emit()
        return nc
    rot = {"mm": 0, "sc": 0, "po": 0}

    def bank(kind):
        base = {"mm": 0, "sc": 2, "po": 4}[kind]
        i = base + (rot[kind] & 1)
        rot[kind] += 1
        return i

    xT_keys = [("xT", c) for c in range(16)]

    wblk = carve([128, 2, 3 * 2048], BF16)
    qT = carve([128, 2048], BF16)
    kT = carve([128, 2048], BF16)
    Vp = carve([128, 16, 130], BF16)
    PT = carve([128, 16, 512], BF16)
    kmf = carve([128, 8], F32)
    kmb = carve([128, 8], BF16)
    gbuf = carve([128, 8], F32)
    m8 = carve([128, 8], F32)
    selall = carve([128, 16, 8], F32)
    Oacc2 = [carve([128, 132], F32) for _ in range(2)]
    rden2 = [carve([128, 1], F32) for _ in range(2)]
    obf2 = [carve([128, 128], BF16) for _ in range(2)]
    mpend = {"n": 0, "p": None}

    def moba_flush():
        if mpend["p"] is None:
            return
        h_, qt_, pq_ = mpend["p"]
        mpend["p"] = None
        P.op("pe", "transpose", out=psb(7, BF16)[:, 0:128], in_=obf2[pq_][:], identity=identb[:],
             reads=[("obf", pq_), "identb"], writes=[("ps", 7)])
        P.op("act", "copy", out=B3[:, h_, qt_ * 128:(qt_ + 1) * 128], in_=psb(7, BF16)[:, 0:128],
             reads=[("ps", 7)], writes=[("actT", qt_)])
    win3 = win_s.rearrange("p (k n) -> p k n", n=2048)
    SCALE_M = 128.0 ** -0.5

    P.op("dve", "memset", ap=Vp[:, :, 128:130], constant=1.0, writes=["Vp_ones"])

    def proj_T(dst, wsl, key_dst, wkey, extra=None):
        w3 = wsl.rearrange("p (c n) -> p c n", n=128)
        for g in range(4):
            b = bank("mm")
            for c in range(16):
                P.op("pe", "matmul", out=psb(b)[:, 0:512], lhsT=w3[:, c, :],
                                                            rhs=A3[:, c, g * 512:(g + 1) * 512],
                                                            start=(c == 0), stop=(c == 15),
                     reads=[("xT", c), wkey], writes=[("ps", b)])
            P.op("act", "copy", out=dst[:, g * 512:(g + 1) * 512], in_=psb(b)[:, 0:512],
                 reads=[("ps", b)], writes=[(key_dst, g)])
            if extra is not None:
                extra(b, g)

    for h in range(8):
        wb = wblk[:, h & 1, :]
        src = win_s.rearrange("p (a b n) -> p a b n", a=6, b=8)[:, 0:3, h, :]
        wkey = ("wblk", h & 1)
        P.dma("sp", wb[:, 0:3 * 2048].rearrange("p (a n) -> p a n", n=2048), src, reads=k_win, writes=[wkey])
        if h == 0:
            ckpt("p1w")
        proj_T(qT, wb[:, 0:2048], "qT", wkey)
        if h == 0:
            ckpt("p1a")

        def km_extra(b, g):
            P.op("dve", "tensor_reduce", out=kmf[:, 2 * g:2 * g + 2], in_=psb(b)[:, 0:512].rearrange("p (a k) -> p a k", k=256),
                axis=AX.X, op=ALU.add, reads=[("ps", b)], writes=["kmf", ("ps", b)])

        proj_T(kT, wb[:, 2048:4096], "kT", wkey, extra=km_extra)
        if h == 0:
            ckpt("p1b")
        P.op("dve", "tensor_scalar", out=kmb[:], in0=kmf[:], scalar1=1.0 / 256.0, scalar2=None,
                                              op0=ALU.mult, reads=["kmf"], writes=["kmb"])
        wv3 = wb[:, 4096:6144].rearrange("p (c n) -> p c n", n=128)
        for k4 in range(4):
            b = bank("mm")
            for j in range(4):
                kt = k4 * 4 + j
                for c in range(16):
                    P.op("pe", "matmul", out=psb(b)[:, j * 128:(j + 1) * 128], lhsT=A3[:, c, kt * 128:(kt + 1) * 128], rhs=wv3[:, c, :],
                        start=(c == 0), stop=(c == 15), reads=[("xT", c), wkey], writes=[("ps", b)])
            P.op("act", "copy", out=Vp[:, k4 * 4:(k4 + 1) * 4, 0:128], in_=psb(b)[:, 0:512].rearrange("p (a n) -> p a n", n=128),
                 reads=[("ps", b)], writes=[("Vp", k4)])
        if h == 0:
            ckpt("p1")
        for qt in range(8, 16):
            qb = qt // 2
            P.op("pe", "matmul", out=psb(6)[:, 0:8], lhsT=qT[:, qt * 128:(qt + 1) * 128], rhs=kmb[:],
                                                 start=True, stop=True,
                 reads=[("qT", qt // 4), "kmb"], writes=[("ps", 6)])
            P.op("dve", "memset", ap=gbuf[:], constant=NEG, writes=["gbuf"])
            P.op("dve", "tensor_copy", out=gbuf[:, 0:qb], in_=psb(6)[:, 0:qb],
                 reads=[("ps", 6)], writes=["gbuf"])
            P.op("dve", "max", out=m8[:], in_=gbuf[:], reads=["gbuf"], writes=["m8"])
            P.op("dve", "tensor_scalar", out=selall[:, qt, :], in0=gbuf[:], scalar1=m8[:, 2:3],
                                                         scalar2=None, op0=ALU.is_ge,
                 reads=["gbuf", "m8"], writes=[("sel", qt)])
        if h == 0:
            ckpt("p1g")
        for g in range(4):
            nk = 4 * g + 4
            for kt in range(nk):
                lo = 0 if kt < 4 * g else (kt - 4 * g) * 128
                b = bank("sc")
                P.op("pe", "matmul", out=psb(b)[:, lo:512], lhsT=kT[:, kt * 128:(kt + 1) * 128], rhs=qT[:, g * 512 + lo:(g + 1) * 512],
                    start=True, stop=True, reads=[("kT", kt // 4), ("qT", g)], writes=[("ps", b)])
                P.op("act", "activation", out=PT[:, kt, lo:512], in_=psb(b)[:, lo:512],
                                                                      func=AF.Exp, scale=SCALE_M,
                     reads=[("ps", b)], writes=[("PT", kt)])
                if kt >= 4 * g:
                    P.op("dve", "tensor_tensor", out=PT[:, kt, lo:lo + 128],
                                                                        in0=PT[:, kt, lo:lo + 128], in1=triLb[:],
                                                                        op=ALU.mult,
                         reads=[("PT", kt), "triLb"], writes=[("PT", kt)])
            for qi in range(4):
                qt = 4 * g + qi
                qb = qt // 2
                pq = mpend["n"] & 1
                mpend["n"] += 1
                Oacc, rden, obf = Oacc2[pq], rden2[pq], obf2[pq]
                kO = ("Oacc", pq)
                for n in range(qb + 1):
                    kts = [k_ for k_ in (2 * n, 2 * n + 1) if k_ <= qt]
                    b = bank("po")
                    for ii, kt in enumerate(kts):
                        P.op("pe", "matmul", out=psb(b)[:, 0:130], lhsT=PT[:, kt, qi * 128:(qi + 1) * 128], rhs=Vp[:, kt, 0:130],
                             start=(ii == 0), stop=(ii == len(kts) - 1),
                             reads=[("PT", kt), ("Vp", kt // 4), "Vp_ones"], writes=[("ps", b)])
                    use_sel = (qb >= 4 and n < qb)
                    if n == 0:
                        if use_sel:
                            P.op("dve", "tensor_scalar", out=Oacc[:, 0:129], in0=psb(b)[:, 0:129], scalar1=selall[:, qt, n:n + 1],
                                 scalar2=None, op0=ALU.mult, reads=[("ps", b), ("sel", qt)], writes=[kO])
                        else:
                            P.op("dve", "tensor_copy", out=Oacc[:, 0:129], in_=psb(b)[:, 0:129],
                                 reads=[("ps", b)], writes=[kO])
                    else:
                        sc = selall[:, qt, n:n + 1] if use_sel else 1.0
                        P.op("dve", "scalar_tensor_tensor", out=Oacc[:, 0:129], in0=psb(b)[:, 0:129], scalar=sc, in1=Oacc[:, 0:129],
                             op0=ALU.mult, op1=ALU.add, reads=[("ps", b), ("sel", qt), kO], writes=[kO])
                P.op("dve", "reciprocal", out=rden[:], in_=Oacc[:, 128:129], reads=[kO], writes=[("rden", pq)])
                P.op("dve", "tensor_scalar", out=obf[:], in0=Oacc[:, 0:128], scalar1=rden[:, 0:1],
                     scalar2=None, op0=ALU.mult, reads=[kO, ("rden", pq)], writes=[("obf", pq)])
                moba_flush()
                mpend["p"] = (h, qt, pq)
    moba_flush()

    ckpt("p1m")
    phase_reset()
    wblk6 = carve([128, 6 * 2048], BF16)
    qT = carve([128, 2048], BF16)
    kT = carve([128, 2048], BF16)
    glT = carve([16, 2048], F32)
    gup = carve([16, 512], F32)
    gbias = carve([1, 512], F32)
    gnorm = carve([128, 256], F32)
    wglb = carve([128, 16, 16], BF16)
    stf = carve([128, 256], F32)
    stb = carve([128, 256], BF16)
    GT = []
    for _p in range(2):
        GT.append(dict(
            spb=carve([128, 128], F32), ksx=carve([128, 128], F32), kstate=carve([128, 128], BF16),
            Ep=carve([128, 128], F32), Em=carve([128, 128], F32), qd=carve([128, 128], BF16),
            kd=carve([128, 128], BF16), vbf=carve([128, 256], BF16), rs=carve([128, 256], F32),
            aTb=carve([128, 128], BF16), junk=carve([128, 256], F32), ss=carve([128, 2], F32),
            og=carve([128, 256], F32), ogb=carve([128, 256], BF16), etmp=carve([128, 128], F32)))

    P.dma("sp", gup[:], gup_d, writes=["gup"])
    P.dma("sp", gbias[:], gbias_d, writes=["gbias"])
    P.dma("sp", gnorm[:], gnorm_d, writes=["gnorm"])
    P.dma("pool", wglb[:].rearrange("p c n -> p (c n)"), wgl_d, writes=["wglb"])
    for g in range(4):
        b = bank("mm")
        for c in range(16):
            P.op("pe", "matmul", out=psb(b)[0:16, 0:512], lhsT=wglb[:, c, :],
                                                        rhs=A3[:, c, g * 512:(g + 1) * 512],
                                                        start=(c == 0), stop=(c == 15),
                 reads=[("xT", c), "wglb"], writes=[("ps", b)])
        P.op("act", "copy", out=glT[:, g * 512:(g + 1) * 512], in_=psb(b)[0:16, 0:512],
             reads=[("ps", b)], writes=["glT"])

    DK_SCALE = 128.0 ** -0.5
    for hg in range(4):
        wb = wblk6
        wkey = "wblk6"
        for i, blk in enumerate([24 + hg, 28 + hg, 32 + 2 * hg, 33 + 2 * hg, 40 + 2 * hg, 41 + 2 * hg]):
            P.dma("sp", wb[:, i * 2048:(i + 1) * 2048], win3[:, blk, :], reads=k_win, writes=[wkey])
        proj_T(qT, wb[:, 0:2048], "qT", wkey)
        proj_T(kT, wb[:, 2048:4096], "kT", wkey)
        wk3 = wb[:, 2048:4096].rearrange("p (c n) -> p c n", n=128)
        wv4 = wb[:, 4096:8192].rearrange("p (a c n) -> p a c n", a=2, n=128)
        wr4 = wb[:, 8192:12288].rearrange("p (a c n) -> p a c n", a=2, n=128)
        P.op("dve", "memset", ap=stf[:], constant=0.0, writes=["stf"])
        P.op("dve", "memset", ap=stb[:], constant=0.0, writes=["stb"])
        def gla_X(tt):
            tsl = slice(tt * 128, (tt + 1) * 128)
            _g = GT[tt & 1]
            spb, ksx, kstate, Ep, Em, qd, kd, vbf, rs = (_g[k_] for k_ in "spb ksx kstate Ep Em qd kd vbf rs".split())
            aTb, junk, ss, og, ogb, etmp = (_g[k_] for k_ in "aTb junk ss og ogb etmp".split())
            pp = tt & 1
            bkv = bank("mm")
            wb6 = wb.rearrange("p (a c n) -> p a c n", a=6, n=128)
            for c in range(16):
                P.op("pe", "matmul", out=psb(bkv)[:, 0:384].rearrange("p (a n) -> p a n", n=128), lhsT=A3[:, c, tsl],
                     rhs=wb6[:, 1:4, c, :], start=(c == 0), stop=(c == 15),
                     reads=[("xT", c), wkey], writes=[("ps", bkv)])
            br = bank("mm")
            for c in range(16):
                P.op("pe", "matmul", out=psb(br)[:, 0:256].rearrange("p (a n) -> p a n", n=128), lhsT=A3[:, c, tsl],
                     rhs=wb6[:, 4:6, c, :], start=(c == 0), stop=(c == 15),
                     reads=[("xT", c), wkey], writes=[("ps", br)])
            bz = bank("sc")
            P.op("pe", "matmul", out=psb(bz)[:, 0:128], lhsT=glT[:, tsl],
                                                               rhs=gup[:, hg * 128:(hg + 1) * 128],
                                                               start=True, stop=False,
                 reads=["glT", "gup"], writes=[("ps", bz)])
            P.op("pe", "matmul", out=psb(bz)[:, 0:128], lhsT=onesf[0:1, :],
                                                      rhs=gbias[:, hg * 128:(hg + 1) * 128], start=False, stop=True,
                 reads=["onesf", "gbias"], writes=[("ps", bz)])
            P.op("act", "activation", out=etmp[:], in_=psb(bz)[:, 0:128], func=AF.Exp, scale=-1.0,
                 reads=[("ps", bz)], writes=[("etmp", pp)])
            P.op("act", "activation", out=spb[:], in_=etmp[:], func=AF.Ln, bias=1.0, scale=1.0,
                 reads=[("etmp", pp)], writes=[("spb", pp)])
            P.op("pe", "matmul", out=psb(bz)[:, 128:256], lhsT=triUf[:], rhs=spb[:], start=True, stop=True,
                 reads=["triUf", ("spb", pp)], writes=[("ps", bz)])
            P.op("pe", "matmul", out=psb(bz)[:, 256:384], lhsT=spb[:], rhs=triLf[:], start=True, stop=True,
                 reads=["triLf", ("spb", pp)], writes=[("ps", bz)])
            P.op("act", "activation", out=ksx[:], in_=psb(bz)[:, 128:256], func=AF.Exp, scale=-1.0 / 16.0,
                 reads=[("ps", bz)], writes=[("ksx", pp)])
            P.op("act", "activation", out=Ep[:], in_=psb(bz)[:, 256:384], func=AF.Exp, scale=-1.0 / 16.0,
                 reads=[("ps", bz)], writes=[("Ep", pp)])
            P.op("act", "activation", out=Em[:], in_=psb(bz)[:, 256:384], func=AF.Exp, scale=1.0 / 16.0,
                 reads=[("ps", bz)], writes=[("Em", pp)])
            P.op("dve", "tensor_tensor", out=kstate[:], in0=psb(bkv)[:, 0:128], in1=ksx[:], op=ALU.mult,
                 reads=[("ps", bkv), ("ksx", pp)], writes=[("kstate", pp)])
            P.op("dve", "scalar_tensor_tensor", out=qd[:], in0=qT[:, tsl], scalar=DK_SCALE, in1=Ep[:],
                                                                  op0=ALU.mult, op1=ALU.mult,
                 reads=[("qT", tt // 4), ("Ep", pp)], writes=[("qd", pp)])
            P.op("dve", "tensor_tensor", out=kd[:], in0=kT[:, tsl], in1=Em[:], op=ALU.mult,
                 reads=[("kT", tt // 4), ("Em", pp)], writes=[("kd", pp)])
            P.op("act", "copy", out=vbf[:], in_=psb(bkv)[:, 128:384], reads=[("ps", bkv)], writes=[("vbf", pp), ("ps", bkv)])
            P.op("act", "activation", out=rs[:], in_=psb(br)[:, 0:256], func=AF.Silu,
                 reads=[("ps", br)], writes=[("rs", pp)])

        def gla_Y(tt):
            tsl = slice(tt * 128, (tt + 1) * 128)
            _g = GT[tt & 1]
            spb, ksx, kstate, Ep, Em, qd, kd, vbf, rs = (_g[k_] for k_ in "spb ksx kstate Ep Em qd kd vbf rs".split())
            aTb, junk, ss, og, ogb, etmp = (_g[k_] for k_ in "aTb junk ss og ogb etmp".split())
            pp = tt & 1
            ba = bank("po")
            P.op("pe", "matmul", out=psb(ba)[:, 0:128], lhsT=kd[:], rhs=qd[:], start=True, stop=True,
                 reads=[("kd", pp), ("qd", pp)], writes=[("ps", ba)])
            P.op("dve", "tensor_tensor", out=aTb[:], in0=psb(ba)[:, 0:128], in1=triLf[:], op=ALU.mult,
                 reads=[("ps", ba), "triLf"], writes=[("aTb", pp)])
            bo = bank("po")
            P.op("pe", "matmul", out=psb(bo)[:, 0:256], lhsT=aTb[:], rhs=vbf[:], start=True, stop=False,
                 reads=[("aTb", pp), ("vbf", pp)], writes=[("ps", bo)])
            P.op("pe", "matmul", out=psb(bo)[:, 0:256], lhsT=qd[:], rhs=stb[:], start=False, stop=True,
                 reads=[("qd", pp), "stb"], writes=[("ps", bo)])
            P.op("pe", "matmul", out=psb(bo)[:, 256:512], lhsT=kstate[:], rhs=vbf[:], start=True, stop=True,
                 reads=[("kstate", pp), ("vbf", pp)], writes=[("ps", bo)])
            P.op("dve", "scalar_tensor_tensor", out=stf[:], in0=stf[:], scalar=Ep[:, 127:128],
                                                               in1=psb(bo)[:, 256:512], op0=ALU.mult, op1=ALU.add,
                 reads=[("ps", bo), ("Ep", pp), "stf"], writes=["stf"])
            P.op("dve", "tensor_copy", out=stb[:], in_=stf[:], reads=["stf"], writes=["stb"])
            P.op("act", "activation", out=junk[:], in_=psb(bo)[:, 0:256], func=AF.Square,
                 reads=[("ps", bo)], writes=[("junk", pp), ("ps", bo)])
            P.op("dve", "tensor_reduce", out=ss[:, 0:1], in_=junk[:], axis=AX.X, op=ALU.add, reads=[("junk", pp)], writes=[("ss", pp)])
            P.op("dve", "tensor_scalar", out=ss[:, 1:2], in0=ss[:, 0:1], scalar1=1.0 / 256.0, scalar2=1e-6,
                                                  op0=ALU.mult, op1=ALU.add, reads=[("ss", pp)], writes=[("ss1", pp)])
            P.op("act", "activation", out=ss[:, 1:2], in_=ss[:, 1:2], func=AF.Sqrt, reads=[("ss1", pp)], writes=[("ss1", pp)])
            P.op("dve", "reciprocal", out=ss[:, 1:2], in_=ss[:, 1:2], reads=[("ss1", pp)], writes=[("ss1", pp)])
            P.op("dve", "scalar_tensor_tensor", out=og[:], in0=psb(bo)[:, 0:256], scalar=ss[:, 1:2],
                                                               in1=gnorm[:], op0=ALU.mult, op1=ALU.mult,
                 reads=[("ps", bo), ("ss1", pp), "gnorm"], writes=[("og", pp)])
            P.op("dve", "tensor_tensor", out=ogb[:], in0=og[:], in1=rs[:], op=ALU.mult,
                 reads=[("og", pp), ("rs", pp)], writes=[("ogb", pp)])
            for a in range(2):
                P.op("pe", "transpose", out=psb(7, BF16)[:, a * 128:(a + 1) * 128],
                                                     in_=ogb[:, a * 128:(a + 1) * 128], identity=identb[:],
                     reads=[("ogb", pp), "identb"], writes=[("ps", 7)])
            P.op("act", "copy", out=B3[:, 8 + 2 * hg:10 + 2 * hg, tsl], in_=psb(7, BF16)[:, 0:256].rearrange("p (a n) -> p a n", n=128),
                 reads=[("ps", 7)], writes=[("actT", tt)])


        for step in range(17):
            if step < 16:
                gla_X(step)
            if step >= 1:
                gla_Y(step - 1)

    ckpt("p1gla")
    LN = {}

    def ln_bufs(nbuf=2):
        LN["n"] = nbuf
        LN["LNP"] = carve([128, 2, 2048], F32)
        LN["RES"] = [carve([128, 2048], F32) for _ in range(nbuf)]
        LN["stats"] = [carve([128, 4, 6], F32) for _ in range(nbuf)]
        LN["mv"] = [carve([128, 4], F32) for _ in range(nbuf)]

    def ln_phase(w_scr, w_keys, res_src, res_key, lnidx, dst_scr, dst_key):
        phase_reset()
        ln_bufs(3)
        LN["ybf"] = [carve([128, 2048], BF16) for _ in range(3)]
        LNP = LN["LNP"]
        P.dma("sp", BUFA[:], w_scr, reads=w_keys, writes=["BUFA"])
        P.dma("sp", LNP[:, 0, :], lnp_d[2 * lnidx], writes=["LNP"])
        P.dma("sp", LNP[:, 1, :], lnp_d[2 * lnidx + 1], writes=["LNP"])
        def res_load(t_):
            if t_ < 16:
                p_ = t_ % LN["n"]
                P.dma("sp", LN["RES"][p_][:], res_src[t_ * 128:(t_ + 1) * 128, :],
                      reads=([(res_key, t_)] if res_key else []), writes=[("RES", p_)])

        res_load(0)
        res_load(1)
        for tt in range(16):
            tsl = slice(tt * 128, (tt + 1) * 128)
            p = tt % LN["n"]
            bs = []
            for ds in range(4):
                b = ds
                bs.append(b)
                for c in range(16):
                    P.op("pe", "matmul", out=psb(b)[:, 0:512], lhsT=B3[:, c, tsl], rhs=A3[:, c, ds * 512:(ds + 1) * 512],
                         start=(c == 0), stop=(c == 15), reads=[("actT", tt), "BUFA"], writes=[("ps", b)])
            if tt > 0:
                ln_tail_b(tt - 1)
            ln_tail(tt, tsl, bs, dst_scr, dst_key, True)
            res_load(tt + 2)
        ln_tail_b(15)

    def ln_tail_b(tt):
        p = tt % LN["n"]
        tsl = slice(tt * 128, (tt + 1) * 128)
        ybf = LN["ybf"][p]
        for c4 in range(4):
            b = 4 + (c4 & 1)
            for j in range(4):
                c = c4 * 4 + j
                P.op("pe", "transpose", out=psb(b, BF16)[:, j * 128:(j + 1) * 128],
                     in_=ybf[:, c * 128:(c + 1) * 128], identity=identb[:],
                     reads=[("ybf", p), "identb"], writes=[("ps", b)])
            P.op("act", "copy", out=B3[:, c4 * 4:(c4 + 1) * 4, tsl],
                 in_=psb(b, BF16)[:, 0:512].rearrange("p (a n) -> p a n", n=128),
                 reads=[("ps", b)], writes=[("actT", tt)])

    def ln_tail(tt, tsl, bs, dst_scr, dst_key, final):
        p = tt % LN["n"]
        LNP, RES, stats, mv = LN["LNP"], LN["RES"][p], LN["stats"][p], LN["mv"][p]
        kR, kS, kM = ("RES", p), ("stats", p), ("mv", p)
        for ds, b in enumerate(bs):
            dsl = slice(ds * 512, (ds + 1) * 512)
            P.op("dve", "scalar_tensor_tensor", out=RES[:, dsl], in0=RES[:, dsl], scalar=ALPHA, in1=psb(b)[:, 0:512],
                 op0=ALU.mult, op1=ALU.add, reads=[("ps", b), kR], writes=[kR])
        for ds, b in enumerate(bs):
            dsl = slice(ds * 512, (ds + 1) * 512)
            P.op("dve", "bn_stats", out=stats[:, ds, :], in_=RES[:, dsl], reads=[kR], writes=[kS])
        P.op("dve", "bn_aggr", out=mv[:, 0:2], in_=stats[:].rearrange("p a b -> p (a b)"),
             reads=[kS], writes=[kM])
        P.op("dve", "tensor_scalar", out=mv[:, 2:3], in0=mv[:, 1:2], scalar1=1e-5, scalar2=None, op0=ALU.add,
             reads=[kM], writes=[kM])
        P.op("act", "activation", out=mv[:, 2:3], in_=mv[:, 2:3], func=AF.Sqrt, reads=[kM], writes=[kM])
        P.op("dve", "reciprocal", out=mv[:, 3:4], in_=mv[:, 2:3], reads=[kM], writes=[kM])
        P.op("dve", "tensor_scalar", out=RES[:], in0=RES[:], scalar1=mv[:, 0:1], scalar2=mv[:, 3:4],
             op0=ALU.subtract, op1=ALU.mult, reads=[kR, kM], writes=[kR])
        P.op("dve", "tensor_tensor", out=RES[:], in0=RES[:], in1=LNP[:, 0, :], op=ALU.mult,
             reads=[kR, "LNP"], writes=[kR])
        P.op("dve", "tensor_tensor", out=RES[:], in0=RES[:], in1=LNP[:, 1, :], op=ALU.add,
             reads=[kR, "LNP"], writes=[kR])
        P.dma("sp", dst_scr[tsl, :], RES[:], reads=[kR], writes=[(dst_key, tt)])
        if "ybf" not in LN or LN["ybf"] is None:
            return
        ybf = LN["ybf"][p]
        P.op("act", "copy", out=ybf[:], in_=RES[:], reads=[kR], writes=[("ybf", p)])
        return
        for c4 in range(4):
            b = 4 + (c4 & 1)
            for j in range(4):
                c = c4 * 4 + j
                P.op("pe", "transpose", out=psb(b, BF16)[:, j * 128:(j + 1) * 128],
                     in_=ybf[:, c * 128:(c + 1) * 128], identity=identb[:],
                     reads=[("ybf", p), "identb"], writes=[("ps", b)])
            P.op("act", "copy", out=B3[:, c4 * 4:(c4 + 1) * 4, tsl],
                 in_=psb(b, BF16)[:, 0:512].rearrange("p (a n) -> p a n", n=128),
                 reads=[("ps", b)], writes=[("actT", tt)])

    ln_phase(wout_s, k_wout, xtok_d, None, 0, out_d if DBG == "h1" else h1_s, "h1")
    if DBG == "h1":
        P.emit()
        return nc

    phase_reset()
    memTb = carve([128, 16, 256], BF16)
    KmT = carve([128, 16, 256], BF16)
    Vm = carve([128, 2, 2048], BF16)
    wkvb = carve([128, 2, 16 * 256], BF16)
    qg = carve([128, 16, 512], BF16)
    PTx = carve([128, 2, 512], BF16)
    xo2 = [carve([128, 512], BF16) for _ in range(2)]
    rdx2 = [carve([128, 2], F32) for _ in range(2)]
    xpend = {"n": 0, "p": None}

    def x_flush():
        if xpend["p"] is None:
            return
        h_, tt_, px_ = xpend["p"]
        xpend["p"] = None
        for j in range(4):
            P.op("pe", "transpose", out=psb(7, BF16)[:, j * 128:(j + 1) * 128],
                 in_=xo2[px_][:, j * 128:(j + 1) * 128], identity=identb[:],
                 reads=[("xo", px_), "identb"], writes=[("ps", 7)])
        P.op("act", "copy", out=B3[:, 4 * h_:4 * h_ + 4, tt_ * 128:(tt_ + 1) * 128],
             in_=psb(7, BF16)[:, 0:512].rearrange("p (a n) -> p a n", n=128),
             reads=[("ps", 7)], writes=[("actT", tt_)])
    for c in range(16):
        P.dma("pool", memTb[:, c, :], memT_d[c * 128:(c + 1) * 128, :], writes=["memTb"])
    wkv3 = wkv_s.rearrange("p (k n) -> p k n", n=16 * 256)
    for blk in range(16):
        wv_ = wkvb[:, blk & 1, :]
        w3 = wv_.rearrange("p (c n) -> p c n", n=256)
        P.dma("sp", wv_, wkv3[:, blk, :], reads=k_wkv, writes=[("wkvb", blk & 1)])
        if blk < 8:
            for j in range(2):
                cc = blk * 2 + j
                b = bank("mm")
                for c in range(16):
                    P.op("pe", "matmul", out=psb(b)[:, 0:256], lhsT=w3[:, c, j * 128:(j + 1) * 128], rhs=memTb[:, c, :],
                         start=(c == 0), stop=(c == 15), reads=["memTb", ("wkvb", blk & 1)], writes=[("ps", b)])
                P.op("act", "copy", out=KmT[:, cc, :], in_=psb(b)[:, 0:256], reads=[("ps", b)], writes=["KmT"])
        else:
            ds = blk - 8
            for mt in range(2):
                b = bank("mm")
                for c in range(16):
                    P.op("pe", "matmul", out=psb(b)[:, 0:256], lhsT=memTb[:, c, mt * 128:(mt + 1) * 128], rhs=w3[:, c, :],
                         start=(c == 0), stop=(c == 15), reads=["memTb", ("wkvb", blk & 1)], writes=[("ps", b)])
                P.op("act", "copy", out=Vm[:, mt, ds * 256:(ds + 1) * 256], in_=psb(b)[:, 0:256],
                     reads=[("ps", b)], writes=["Vm"])
    ckpt("p3kv")
    P.dma("sp", BUFA[:], wq_s, reads=k_wq, writes=["BUFA"])
    SCALE_X = 512.0 ** -0.5
    for g in range(4):
        gsl = slice(g * 512, (g + 1) * 512)
        gkeys = [("actT", 4 * g + i) for i in range(4)]
        for cc in range(16):
            b = bank("mm")
            for c in range(16):
                P.op("pe", "matmul", out=psb(b)[:, 0:512], lhsT=A3[:, c, cc * 128:(cc + 1) * 128], rhs=B3[:, c, gsl],
                    start=(c == 0), stop=(c == 15), reads=gkeys + ["BUFA"], writes=[("ps", b)])
            P.op("act", "copy", out=qg[:, cc, :], in_=psb(b)[:, 0:512],
                 reads=[("ps", b)], writes=[("qg", cc // 4)])
        for h in range(4):
            for mt in range(2):
                b = bank("sc")
                for j in range(4):
                    P.op("pe", "matmul", out=psb(b)[:, 0:512], lhsT=KmT[:, 4 * h + j, mt * 128:(mt + 1) * 128], rhs=qg[:, 4 * h + j, :],
                        start=(j == 0), stop=(j == 3), reads=["KmT", ("qg", h)], writes=[("ps", b)])
                P.op("act", "activation", out=PTx[:, mt, :], in_=psb(b)[:, 0:512], func=AF.Exp,
                                                               scale=SCALE_X, reads=[("ps", b)], writes=[("PTx", mt)])
            for ti in range(4):
                tt = 4 * g + ti
                b = bank("po")
                for mt in range(2):
                    P.op("pe", "matmul", out=psb(b)[:, 0:512], lhsT=PTx[:, mt, ti * 128:(ti + 1) * 128], rhs=Vm[:, mt, h * 512:(h + 1) * 512],
                        start=(mt == 0), stop=(mt == 1), reads=[("PTx", mt), "Vm"], writes=[("ps", b)])
                for mt in range(2):
                    P.op("pe", "matmul", out=psb(6)[:, 0:2], lhsT=PTx[:, mt, ti * 128:(ti + 1) * 128], rhs=onesb[:, 0:2],
                        start=(mt == 0), stop=(mt == 1), reads=[("PTx", mt), "onesb"], writes=[("ps", 6)])
                px = xpend["n"] & 1
                xpend["n"] += 1
                P.op("dve", "reciprocal", out=rdx2[px][:, 0:1], in_=psb(6)[:, 0:1], reads=[("ps", 6)], writes=[("rdx", px)])
                P.op("dve", "tensor_scalar", out=xo2[px][:], in0=psb(b)[:, 0:512], scalar1=rdx2[px][:, 0:1],
                     scalar2=None, op0=ALU.mult, reads=[("ps", b), ("rdx", px)], writes=[("xo", px)])
                x_flush()
                xpend["p"] = (h, tt, px)
    x_flush()

    ckpt("p3a")
    ln_phase(wo_s, k_wo, h1_s, "h1", 1, out_d if DBG == "h2" else h2_s, "h2")
    if DBG == "h2":
        P.emit()
        return nc

    phase_reset()
    IJ = carve([128, 16, 2, 128], BF16)
    GG = carve([128, 16, 128], F32)
    off_keep = wk["off"]
    skTb = carve([128, 16, 128], BF16)
    qpT = carve([128, 16, 128], BF16)
    sbuf_s = carve([128, 16, 128], F32)
    s2 = carve([128, 16, 128], F32)
    vals = carve([128, 16, 16], F32)
    idxu = carve([128, 16, 16], U32)
    idxf2 = [carve([128, 16, 16], F32) for _ in range(2)]
    cand = carve([128, 8, 256], F32)
    c2 = carve([128, 8, 256], F32)
    best2 = [carve([128, 8, 16], F32) for _ in range(2)]
    posu2 = [carve([128, 8, 16], U32) for _ in range(2)]
    gt0 = carve([128, 8, 16], F32)
    gt1 = carve([128, 8, 16], F32)
    posf = carve([128, 8, 16], F32)
    thr16 = carve([128, 16], F32)
    pa_f = carve([128, 8, 16], F32)
    pb_f = carve([128, 8, 16], F32)
    oh = carve([128, 8, 16, 16], F32)
    zz = carve([128, 8], F32)
    P.op("dve", "tensor_scalar", out=thr16[:], in0=iotaf[:, 0:16], scalar1=1.0, scalar2=16.0, op0=ALU.add, op1=ALU.mult,
         reads=["iotaf"], writes=["thr16"])
    P.dma("pool", skTb[:].rearrange("p c n -> p (c n)"), skT_d, writes=["skTb"])
    P.dma("sp", BUFA[:], pwq_s, reads=k_pwq, writes=["BUFA"])
    io16 = iotaf[:, 0:16]
    prev_tail = []
    for tt in range(16):
        tsl = slice(tt * 128, (tt + 1) * 128)
        for c4 in range(4):
            b = bank("mm")
            for j in range(4):
                cc = c4 * 4 + j
                for c in range(16):
                    P.op("pe", "matmul", out=psb(b)[:, j * 128:(j + 1) * 128], lhsT=A3[:, c, cc * 128:(cc + 1) * 128], rhs=B3[:, c, tsl],
                        start=(c == 0), stop=(c == 15), reads=[("actT", tt), "BUFA"], writes=[("ps", b)])
            P.op("act", "copy", out=qpT[:, c4 * 4:(c4 + 1) * 4, :],
                                                     in_=psb(b)[:, 0:512].rearrange("p (a n) -> p a n", n=128),
                 reads=[("ps", b)], writes=[("qpT", c4)])
        for c4 in range(4):
            b = bank("sc")
            for j in range(4):
                cc = c4 * 4 + j
                P.op("pe", "matmul", out=psb(b)[:, j * 128:(j + 1) * 128], lhsT=qpT[:, cc, :],
                                                              rhs=skTb[:, cc, :], start=True, stop=True,
                     reads=[("qpT", c4), "skTb"], writes=[("ps", b)])
            P.op("act", "copy", out=sbuf_s[:, c4 * 4:(c4 + 1) * 4, :],
                                                     in_=psb(b)[:, 0:512].rearrange("p (a n) -> p a n", n=128),
                 reads=[("ps", b)], writes=[("s", c4)])
        pt = tt & 1
        idxf, best, posu = idxf2[pt], best2[pt], posu2[pt]
        P.capture = []
        for cc in range(16):
            P.op("dve", "max", out=vals[:, cc, 0:8], in_=sbuf_s[:, cc, :], reads=[("s", cc // 4)], writes=[("vals", cc, 0)])
        for cc in range(16):
            P.op("dve", "match_replace", out=s2[:, cc, :], in_to_replace=vals[:, cc, 0:8], in_values=sbuf_s[:, cc, :],
                 imm_value=NEG, reads=[("s", cc // 4), ("vals", cc, 0)], writes=[("s2", cc)])
        for cc in range(16):
            P.op("dve", "max", out=vals[:, cc, 8:16], in_=s2[:, cc, :], reads=[("s2", cc)], writes=[("vals", cc, 1)])
        for cc in range(16):
            P.op("dve", "max_index", out=idxu[:, cc, 0:8], in_max=vals[:, cc, 0:8], in_values=sbuf_s[:, cc, :],
                 reads=[("s", cc // 4), ("vals", cc, 0)], writes=[("idxu", cc, 0)])
        for cc in range(16):
            P.op("dve", "max_index", out=idxu[:, cc, 8:16], in_max=vals[:, cc, 8:16], in_values=sbuf_s[:, cc, :],
                 reads=[("s", cc // 4), ("vals", cc, 1)], writes=[("idxu", cc, 1)])
        k_vals = [("vals", cc, hh) for cc in range(16) for hh in range(2)]
        k_idxu = [("idxu", cc, hh) for cc in range(16) for hh in range(2)]
        P.op("dve", "tensor_copy", out=idxf[:], in_=idxu[:], reads=k_idxu, writes=[("idxf", pt)])
        v4 = vals[:].rearrange("p (h two) k -> p h two k", two=2)
        i4 = idxf[:].rearrange("p (h two) k -> p h two k", two=2)
        cand4 = cand[:].rearrange("p h (a b) -> p h a b", b=16)
        P.op("dve", "tensor_tensor", out=cand4, in0=v4[:, :, 0, :].unsqueeze(3).to_broadcast([128, 8, 16, 16]),
             in1=v4[:, :, 1, :].unsqueeze(2).to_broadcast([128, 8, 16, 16]), op=ALU.add, reads=k_vals, writes=["cand"])
        for h in range(8):
            P.op("dve", "max", out=best[:, h, 0:8], in_=cand[:, h, :], reads=["cand"], writes=[("best", pt, h, 0)])
        for h in range(8):
            P.op("dve", "match_replace", out=c2[:, h, :], in_to_replace=best[:, h, 0:8], in_values=cand[:, h, :],
                 imm_value=NEG, reads=["cand", ("best", pt, h, 0)], writes=[("c2", h)])
        for h in range(8):
            P.op("dve", "max", out=best[:, h, 8:16], in_=c2[:, h, :], reads=[("c2", h)], writes=[("best", pt, h, 1)])
        for h in range(8):
            P.op("dve", "max_index", out=posu[:, h, 0:8], in_max=best[:, h, 0:8], in_values=cand[:, h, :],
                 reads=["cand", ("best", pt, h, 0)], writes=[("posu", pt, h, 0)])
        for h in range(8):
            P.op("dve", "max_index", out=posu[:, h, 8:16], in_max=best[:, h, 8:16], in_values=cand[:, h, :],
                 reads=["cand", ("best", pt, h, 1)], writes=[("posu", pt, h, 1)])
        k_best = [("best", pt, h, hh) for h in range(8) for hh in range(2)]
        k_posu = [("posu", pt, h, hh) for h in range(8) for hh in range(2)]
        capA = P.capture
        P.capture = []
        gsl_ = GG[:, tt, :].rearrange("p (h k) -> p h k", k=16)
        P.op("dve", "tensor_tensor", out=gt0[:], in0=best[:], in1=best[:, :, 0:1].to_broadcast([128, 8, 16]), op=ALU.subtract,
             reads=k_best, writes=["gt0"])
        P.op("act", "activation", out=gt1[:], in_=gt0[:], func=AF.Exp, reads=["gt0"], writes=["gt1"])
        P.op("dve", "tensor_reduce", out=zz[:], in_=gt1[:], axis=AX.X, op=ALU.add, reads=["gt1"], writes=["zz"])
        P.op("dve", "reciprocal", out=zz[:], in_=zz[:], reads=["zz"], writes=["zz"])
        P.op("dve", "tensor_tensor", out=gsl_, in0=gt1[:], in1=zz[:].unsqueeze(2).to_broadcast([128, 8, 16]), op=ALU.mult,
             reads=["gt1", "zz"], writes=[("GG", tt)])
        P.op("dve", "tensor_copy", out=posf[:], in_=posu[:], reads=k_posu, writes=["posf"])
        P.op("dve", "tensor_tensor", out=oh[:], in0=posf[:].unsqueeze(3).to_broadcast([128, 8, 16, 16]),
             in1=thr16[:].unsqueeze(1).unsqueeze(1).to_broadcast([128, 8, 16, 16]), op=ALU.is_ge,
             reads=["posf", "thr16"], writes=["oh0", "oh1", "oh"])
        P.op("dve", "tensor_reduce", out=pa_f[:], in_=oh[:], axis=AX.X, op=ALU.add, reads=["oh"], writes=["pa_f"])
        P.op("dve", "scalar_tensor_tensor", out=pb_f[:], in0=pa_f[:], scalar=-16.0, in1=posf[:], op0=ALU.mult, op1=ALU.add,
             reads=["pa_f", "posf"], writes=["pb_f"])
        for which, pf in ((0, pa_f), (1, pb_f)):
            P.op("dve", "tensor_tensor", out=oh[:], in0=pf[:].unsqueeze(3).to_broadcast([128, 8, 16, 16]),
                in1=io16.unsqueeze(1).unsqueeze(1).to_broadcast([128, 8, 16, 16]), op=ALU.is_equal,
                 reads=["pa_f", "pb_f", "iotaf"], writes=["oh0", "oh1", "oh"])
            P.op("dve", "tensor_tensor", out=oh[:], in0=oh[:], in1=i4[:, :, which, :].unsqueeze(2).to_broadcast([128, 8, 16, 16]), op=ALU.mult,
                 reads=["oh", ("idxf", pt)], writes=["oh"])
            P.op("dve", "tensor_reduce", out=IJ[:, tt, which, :].rearrange("p (h k) -> p h k", k=16), in_=oh[:], axis=AX.X, op=ALU.add,
                 reads=["oh"], writes=[("IJ", tt)])
        capB = P.capture
        P.capture = None
        g_ops, i_ops = capB[:5], capB[5:]
        tail = []
        for k_ in range(max(len(g_ops), len(i_ops))):
            if k_ < len(i_ops):
                tail.append(i_ops[k_])
            if k_ < len(g_ops):
                tail.append(g_ops[k_])
        P.replay(capA, prev_tail)
        prev_tail = tail
    P.replay(prev_tail)

    ckpt("p4a")
    P.barrier()
    wk["off"] = off_keep
    ln_bufs(1)
    LN["ybf"] = None
    LNP, RES = LN["LNP"], LN["RES"][0]
    ItJt = carve([128, 2, 256], F32)
    gtT = carve([128, 256], F32)
    OIg = carve([128, 16, 128], BF16)
    OJ = carve([128, 2, 16 * 128], BF16)
    NRING = 4
    wring = carve([128, NRING, 2048], BF16)
    gel = carve([128, 2, 256], F32)
    IJf = carve([128, 2, 128], F32)
    ring = {"n": 0}
    P.dma("sp", LNP[:, 0, :], lnp_d[4], writes=["LNP"])
    P.dma("sp", LNP[:, 1, :], lnp_d[5], writes=["LNP"])
    G3 = BUFA[:].rearrange("p (j t) -> p j t", t=256)
    Gkeys = [("G", j) for j in range(128)]
    UT3 = UT_s.rearrange("(j p) n -> j p n", p=128)
    V3 = V_s.rearrange("(j p) n -> j p n", p=128)
    for grp in range(8):
        for u in range(2):
            tt = 2 * grp + u
            P.op("dve", "tensor_copy", out=IJf[:], in_=IJ[:, tt, :, :], reads=[("IJ", tt)], writes=["IJf"])
            for w_ in range(2):
                P.op("pe", "transpose", out=psb(6)[:, w_ * 128:(w_ + 1) * 128], in_=IJf[:, w_, :],
                                                       identity=identf[:], reads=["IJf", "identf"], writes=[("ps", 6)])
            P.op("pe", "transpose", out=psb(6)[:, 256:384], in_=GG[:, tt, :], identity=identf[:],
                 reads=[("GG", tt), "identf"], writes=[("ps", 6)])
            P.op("act", "copy", out=ItJt[:, :, u * 128:(u + 1) * 128],
                                              in_=psb(6)[:, 0:256].rearrange("p (a n) -> p a n", n=128),
                 reads=[("ps", 6)], writes=["ItJt"])
            P.op("act", "copy", out=gtT[:, u * 128:(u + 1) * 128], in_=psb(6)[:, 256:384],
                 reads=[("ps", 6)], writes=["gtT"])
        for t16 in range(16):
            t0 = t16 * 16
            pj = t16 & 1
            OJ3 = OJ[:, pj, :].rearrange("p (t j) -> p t j", j=128)
            P.op("dve", "tensor_tensor", out=OJ3, in0=iotaf[:].unsqueeze(1).to_broadcast([128, 16, 128]),
                 in1=ItJt[:, 1, t0:t0 + 16].unsqueeze(2).to_broadcast([128, 16, 128]), op=ALU.is_equal,
                 reads=["ItJt", "iotaf"], writes=[("OJ", pj)])
            for tl in range(16):
                P.op("dve", "tensor_scalar", out=OIg[:, tl, :], in0=iotaf[:], scalar1=ItJt[:, 0, t0 + tl:t0 + tl + 1],
                     scalar2=gtT[:, t0 + tl:t0 + tl + 1], op0=ALU.is_equal, op1=ALU.mult,
                     reads=["ItJt", "gtT", "iotaf"], writes=[("OIg", tl)])
            bkeys = [("ps", pj * 4 + q4) for q4 in range(4)]
            for tl in range(16):
                b = pj * 4 + tl // 4
                P.op("pe", "matmul", out=psb(b)[:, (tl % 4) * 128:(tl % 4 + 1) * 128], lhsT=OIg[:, tl, :],
                     rhs=OJ3[:, tl, :], start=True, stop=True,
                     reads=[("OIg", tl), ("OJ", pj)], writes=[("ps", b)])
            P.op("act", "copy", out=G3[:, :, t0:t0 + 16],
                 in_=PSALL[:, pj * 2048:(pj + 1) * 2048].rearrange("p (t j) -> p j t", j=128),
                 reads=bkeys, writes=Gkeys)
        if grp == 0:
            ckpt("p4g")
        gsl = slice(grp * 256, (grp + 1) * 256)
        akeys = [("actT", 2 * grp), ("actT", 2 * grp + 1)]
        for j in range(128):
            slot = ring["n"] % NRING
            ring["n"] += 1
            ub = wring[:, slot, :]
            u3 = ub.rearrange("p (c n) -> p c n", n=128)
            P.dma("sp", ub, UT3[j], reads=[k_UT[j // 16]], writes=[("wring", slot)])
            b = bank("sc")
            for c in range(16):
                P.op("pe", "matmul", out=psb(b)[:, 0:256], lhsT=u3[:, c, :], rhs=B3[:, c, gsl],
                                                                        start=(c == 0), stop=(c == 15),
                     reads=akeys + [("wring", slot)], writes=[("ps", b)])
            P.op("act", "activation", out=gel[:, j & 1, :], in_=psb(b)[:, 0:256], func=AF.Gelu,
                 reads=[("ps", b)], writes=[("gel", j & 1)])
            P.op("dve", "tensor_tensor", out=G3[:, j, :], in0=G3[:, j, :], in1=gel[:, j & 1, :], op=ALU.mult,
                 reads=[("gel", j & 1), ("G", j)], writes=[("G", j)])
        if grp == 0:
            ckpt("p4A")
        for j in range(128):
            slot = ring["n"] % NRING
            ring["n"] += 1
            vb = wring[:, slot, :]
            P.dma("sp", vb, V3[j], reads=[k_V[j // 16]], writes=[("wring", slot)])
            for u in range(2):
                for ds in range(4):
                    b = u * 4 + ds
                    P.op("pe", "matmul", out=psb(b)[:, 0:512], lhsT=G3[:, j, u * 128:(u + 1) * 128], rhs=vb[:, ds * 512:(ds + 1) * 512],
                        start=(j == 0), stop=(j == 127), reads=[("G", j), ("wring", slot)], writes=[("ps", b)])
        for u in range(2):
            tt = 2 * grp + u
            tsl = slice(tt * 128, (tt + 1) * 128)
            P.dma("sp", RES[:], h2_s[tsl, :], reads=[("h2", tt)], writes=[("RES", 0)])
            ln_tail(tt, tsl, [u * 4 + ds for ds in range(4)], out_d, "out", True)
    P.emit()
    return nc


def _prep_inputs(inp):
    f = np.float32
    w_in = np.asarray(inp["w_in"][0], f)
    com = {}
    com["w_in"] = np.ascontiguousarray(
        w_in[:, :6144].reshape(16, 128, 48, 128).transpose(1, 2, 0, 3)).reshape(128, 48 * 2048)
    com["w_gl"] = np.ascontiguousarray(w_in[:, 6144:].reshape(16, 128, 16).transpose(1, 0, 2)).reshape(128, 256)
    com["gate_up"] = np.ascontiguousarray(inp["gla_gate_up"][0], f)
    com["gate_bias"] = np.ascontiguousarray(inp["gla_gate_bias"][0], f).reshape(1, 512)
    com["gnorm"] = np.ascontiguousarray(np.broadcast_to(np.asarray(inp["gla_norm_g"][0], f)[None, :], (128, 256)))

    def kmaj(w):
        w = np.asarray(w, f)
        return np.ascontiguousarray(w.reshape(16, 128, w.shape[1]).transpose(1, 0, 2)).reshape(128, -1)

    com["w_out"] = kmaj(inp["w_out"][0])
    com["wq"] = kmaj(inp["xattn_wq"][0])
    wkv = np.asarray(inp["xattn_wkv"][0], f)
    com["wkv"] = np.ascontiguousarray(wkv.reshape(16, 128, 16, 256).transpose(1, 2, 0, 3)).reshape(128, 16 * 16 * 256)
    com["wo"] = kmaj(inp["xattn_wo"][0])
    com["pwq"] = kmaj(inp["peer_wq"][0])
    sk = np.asarray(inp["peer_sub_keys"][0], f)
    com["skT"] = np.ascontiguousarray(sk.reshape(16, 128, 128).transpose(2, 0, 1)).reshape(128, 16 * 128)
    U = np.asarray(inp["peer_u"][0], f)
    com["UT"] = np.ascontiguousarray(U.reshape(128, 128, 16, 128).transpose(1, 3, 2, 0)).reshape(128 * 128, 2048)
    Vt = np.asarray(inp["peer_v"][0], f)
    com["Vp"] = np.ascontiguousarray(Vt.reshape(128, 128, 2048).transpose(1, 0, 2)).reshape(128 * 128, 2048)
    lnp = np.stack([np.asarray(inp[k][0], f) for k in ("ln1_g", "ln1_b", "ln2_g", "ln2_b", "ln3_g", "ln3_b")])
    com["lnp"] = np.ascontiguousarray(np.broadcast_to(lnp[:, None, :], (6, 128, 2048)))
    com["c_ident"] = np.eye(128, dtype=f)
    r = np.arange(128)
    com["c_triL"] = (r[:, None] <= r[None, :]).astype(f)
    com["c_triU"] = (r[:, None] > r[None, :]).astype(f)
    com["c_iota"] = np.ascontiguousarray(np.broadcast_to(r[None, :].astype(f), (128, 128)))
    maps = []
    x = np.asarray(inp["x"], f)
    mem = np.asarray(inp["mem"], f)
    for b in range(8):
        m = dict(com)
        m["xT"] = np.ascontiguousarray(x[b].T)
        m["xtok"] = np.ascontiguousarray(x[b])
        m["memT"] = np.ascontiguousarray(mem[b].T)
        maps.append(m)
    return maps


_NC_CACHE = {}


def kernel(**inputs):
    maps = _prep_inputs(inputs)
    if "nc" not in _NC_CACHE:
        _NC_CACHE["nc"] = build_program()
    nc = _NC_CACHE["nc"]
    res = run_bass_kernel_spmd(nc, maps, core_ids=list(range(8)))
    out = np.stack([np.asarray(r["out"], np.float32) for r in res.results], axis=0)
    return out
```

```python
import numpy as np
import concourse.bass as bass
import concourse.mybir as mybir
from concourse.bass_utils import run_bass_kernel_spmd

F32 = mybir.dt.float32
BF16 = mybir.dt.bfloat16
U32 = mybir.dt.uint32
AF = mybir.ActivationFunctionType
ALU = mybir.AluOpType
AX = mybir.AxisListType

S = 2048
D = 2048
NT = 16
ALPHA = 2.0 ** 0.25
NEG = -1.0e30
ENGS = ("sp", "pe", "act", "dve", "pool")
DBG = None


class Prog:
    def __init__(self, nc):
        self.nc = nc
        self.ops = {e: [] for e in ENGS}
        self.cnt = {e: 0 for e in ENGS}
        self.lastw = {}
        self.readers = {}
        self.waited = {e: {} for e in ENGS}
        self.dpool = {"sp": ["ds%d" % i for i in range(20)], "pool": ["dp%d" % i for i in range(12)],
                      "act": ["da%d" % i for i in range(8)]}
        self.drr = {"sp": 0, "pool": 0, "act": 0}
        self.dtot = {}
        for q in self.dpool:
            for n in self.dpool[q]:
                self.dtot[n] = 0

    def _need(self, eng, reads, writes):
        need = {}

        def add(sv):
            if sv is None:
                return
            s, v = sv
            if need.get(s, 0) < v:
                need[s] = v

        for k in reads:
            add(self.lastw.get(k))
        for k in writes:
            add(self.lastw.get(k))
            for s, v in self.readers.get(k, {}).items():
                add((s, v))
        for s, v in need.items():
            if eng == "pe" and s == "pe":
                continue
            if self.waited[eng].get(s, 0) < v:
                self.waited[eng][s] = v
                self.ops[eng].append(("w", s, v))

    def _mark(self, sem, val, reads, writes):
        for k in reads:
            self.readers.setdefault(k, {})[sem] = val
        for k in writes:
            self.lastw[k] = (sem, val)
            self.readers[k] = {}

    capture = None

    def op(self, eng, meth, reads=(), writes=(), **kw):
        if self.capture is not None:
            self.capture.append((eng, meth, list(reads), list(writes), kw))
            return
        self._need(eng, reads, writes)
        self.cnt[eng] += 1
        self.ops[eng].append(("i", meth, kw))
        self._mark(eng, self.cnt[eng], reads, writes)

    def dma(self, q, out, in_, reads=(), writes=(), **kw):
        self._need(q, reads, writes)
        pool = self.dpool[q]
        name = pool[self.drr[q] % len(pool)]
        self.drr[q] += 1
        prev = self.dtot[name]
        if prev and self.waited[q].get(name, 0) < prev:
            self.waited[q][name] = prev
            self.ops[q].append(("w", name, prev))
        self.dtot[name] = prev + 16
        self.ops[q].append(("d", lambda e: e.dma_start(out=out, in_=in_, **kw), name))
        self._mark(name, prev + 16, reads, writes)
        if q == "pool":
            self.waited[q][name] = prev + 16
            self.ops[q].append(("w", name, prev + 16))

    def replay(self, A, B=()):
        A, B = list(A), list(B)
        step = max(1, len(A) // (len(B) + 1)) if B else 0
        bi = 0
        for i, o in enumerate(A):
            self.op(o[0], o[1], o[2], o[3], **o[4])
            if B and (i + 1) % step == 0 and bi < len(B):
                o2 = B[bi]
                bi += 1
                self.op(o2[0], o2[1], o2[2], o2[3], **o2[4])
        for o2 in B[bi:]:
            self.op(o2[0], o2[1], o2[2], o2[3], **o2[4])

    def barrier(self):
        for e in ENGS:
            for o in ENGS:
                if self.cnt[o] and self.waited[e].get(o, 0) < self.cnt[o]:
                    self.waited[e][o] = self.cnt[o]
                    self.ops[e].append(("w", o, self.cnt[o]))
            for n, v in self.dtot.items():
                if v and self.waited[e].get(n, 0) < v:
                    self.waited[e][n] = v
                    self.ops[e].append(("w", n, v))

    def emit(self):
        nc = self.nc
        sems = {}
        for e in ENGS:
            sems[e] = nc.alloc_semaphore("sem_" + e)
        for n in self.dtot:
            sems[n] = nc.alloc_semaphore("sem_" + n)
        self.barrier()

        def run(eng, key):
            for o in self.ops[key]:
                if o[0] == "w":
                    eng.wait_ge(sems[o[1]], o[2])
                elif o[0] == "i":
                    getattr(eng, o[1])(**o[2]).then_inc(sems[key], 1)
                else:
                    o[1](eng).then_inc(sems[o[2]], 16)

        with nc.allow_low_precision("small integer indices / 0-1 masks are exact in bf16"), nc.Block() as block:
            @block.sync
            def _(e):
                run(e, "sp")

            @block.tensor
            def _(e):
                run(e, "pe")

            @block.scalar
            def _(e):
                run(e, "act")

            @block.vector
            def _(e):
                run(e, "dve")

            @block.gpsimd
            def _(e):
                run(e, "pool")


class _Stop(Exception):
    pass


def build_program():
    nc = bass.Bass("TRN2", target_bir_lowering=False)
    try:
        _build(nc)
    except _Stop:
        pass
    return nc


def _build(nc):
    P = Prog(nc)

    def ckpt(name):
        if DBG == name:
            P.barrier()
            for tt_ in range(16):
                P.dma("sp", out_d[tt_ * 128:(tt_ + 1) * 128, :], xtok_d[tt_ * 128:(tt_ + 1) * 128, :])
            P.emit()
            raise _Stop()

    def din(name, shape, dt=F32):
        return nc.dram_tensor(name, shape, dt, kind="ExternalInput").ap()

    def dscr(name, shape, dt):
        return nc.dram_tensor(name, shape, dt, kind="Internal").ap()

    xT_d = din("xT", [2048, 2048])
    xtok_d = din("xtok", [2048, 2048])
    memT_d = din("memT", [2048, 256])
    win_d = din("w_in", [128, 48 * 2048])
    wgl_d = din("w_gl", [128, 256])
    gup_d = din("gate_up", [16, 512])
    gbias_d = din("gate_bias", [1, 512])
    gnorm_d = din("gnorm", [128, 256])
    wout_d = din("w_out", [128, 16 * 2048])
    wq_d = din("wq", [128, 16 * 2048])
    wkv_d = din("wkv", [128, 16 * 16 * 256])
    wo_d = din("wo", [128, 16 * 2048])
    pwq_d = din("pwq", [128, 16 * 2048])
    skT_d = din("skT", [128, 16 * 128])
    NUV = 128 * 128 if DBG is None else 128
    UT_d = din("UT", [NUV, 2048])
    V_d = din("Vp", [NUV, 2048])
    lnp_d = din("lnp", [6, 128, 2048])
    c_ident = din("c_ident", [128, 128])
    c_triL = din("c_triL", [128, 128])
    c_triU = din("c_triU", [128, 128])
    c_iota = din("c_iota", [128, 128])
    out_d = nc.dram_tensor("out", [2048, 2048], F32, kind="ExternalOutput").ap()

    win_s = dscr("win_s", [128, 48 * 2048], BF16)
    wout_s = dscr("wout_s", [128, 16 * 2048], BF16)
    wq_s = dscr("wq_s", [128, 16 * 2048], BF16)
    wkv_s = dscr("wkv_s", [128, 16 * 16 * 256], BF16)
    wo_s = dscr("wo_s", [128, 16 * 2048], BF16)
    pwq_s = dscr("pwq_s", [128, 16 * 2048], BF16)
    UT_s = dscr("UT_s", [NUV, 2048], BF16)
    V_s = dscr("V_s", [NUV, 2048], BF16)
    h1_s = dscr("h1_s", [2048, 2048], F32)
    h2_s = dscr("h2_s", [2048, 2048], F32)

    BUFA = nc.alloc_sbuf_tensor("BUFA", [128, 32768], BF16)
    BUFB = nc.alloc_sbuf_tensor("BUFB", [128, 32768], BF16)
    identb = nc.alloc_sbuf_tensor("identb", [128, 128], BF16)
    identf = nc.alloc_sbuf_tensor("identf", [128, 128], F32)
    triLb = nc.alloc_sbuf_tensor("triLb", [128, 128], BF16)
    triLf = nc.alloc_sbuf_tensor("triLf", [128, 128], F32)
    triUf = nc.alloc_sbuf_tensor("triUf", [128, 128], F32)
    iotaf = nc.alloc_sbuf_tensor("iotaf", [128, 128], F32)
    onesb = nc.alloc_sbuf_tensor("onesb", [128, 2], BF16)
    onesf = nc.alloc_sbuf_tensor("onesf", [128, 128], F32)
    WORKN = 19200
    WORK = nc.alloc_sbuf_tensor("WORK", [128, WORKN], F32)
    PSALL = nc.alloc_psum_tensor("psall", [128, 4096], F32)

    wk = {"off": 0}

    def carve(shape, dt):
        n = 1
        for s_ in shape[1:]:
            n *= s_
        words = n if dt in (F32, U32) else (n + 1) // 2
        words = (words + 7) // 8 * 8
        o = wk["off"]
        assert o + words <= WORKN, ("WORK overflow", o, words)
        wk["off"] = o + words
        v = WORK[0:shape[0], o:o + words]
        if dt == BF16:
            v = v.bitcast(BF16)[:, 0:n]
        elif dt == U32:
            v = v.bitcast(U32)[:, 0:n]
        else:
            v = v[:, 0:n]
        if len(shape) == 3:
            v = v.rearrange("p (a b) -> p a b", b=shape[2])
        elif len(shape) == 4:
            v = v.rearrange("p (a b c) -> p a b c", b=shape[2], c=shape[3])
        return v

    def phase_reset():
        P.barrier()
        wk["off"] = 0

    A3 = BUFA[:].rearrange("p (c t) -> p c t", t=2048)
    B3 = BUFB[:].rearrange("p (c t) -> p c t", t=2048)

    def psb(i, dt=F32):
        v = PSALL[:, i * 512:(i + 1) * 512]
        return v if dt == F32 else v.bitcast(BF16)

    def cast_w(name, dst, src, rows, cols, parts=1):
        step = rows // parts
        b = min(cols, 2048)
        for i in range(parts):
            s_ = src[i * step:(i + 1) * step, :].rearrange("p (a b) -> p a b", b=b)
            d_ = dst[i * step:(i + 1) * step, :].rearrange("p (a b) -> p a b", b=b)
            P.dma("pool", d_, s_, writes=[(name, i)])
        return [(name, i) for i in range(parts)]

    P.dma("sp", identf[:], c_ident, writes=["identf"])
    P.dma("sp", triLf[:], c_triL, writes=["triLf"])
    P.dma("sp", triUf[:], c_triU, writes=["triUf"])
    P.dma("sp", iotaf[:], c_iota, writes=["iotaf"])
    P.op("dve", "tensor_copy", out=identb[:], in_=identf[:], reads=["identf"], writes=["identb"])
    P.op("dve", "tensor_copy", out=triLb[:], in_=triLf[:], reads=["triLf"], writes=["triLb"])
    P.op("dve", "memset", ap=onesb[:], constant=1.0, writes=["onesb"])
    P.op("dve", "memset", ap=onesf[:], constant=1.0, writes=["onesf"])

    for c in range(16):
        P.dma("pool", A3[:, c, :], xT_d[c * 128:(c + 1) * 128, :], writes=[("xT", c)])
    k_win = cast_w("win_s", win_s, win_d, 128, 48 * 2048, parts=1)
    k_wout = cast_w("wout_s", wout_s, wout_d, 128, 16 * 2048)
    k_wkv = cast_w("wkv_s", wkv_s, wkv_d, 128, 16 * 16 * 256)
    k_wq = cast_w("wq_s", wq_s, wq_d, 128, 16 * 2048)
    k_wo = cast_w("wo_s", wo_s, wo_d, 128, 16 * 2048)
    k_pwq = cast_w("pwq_s", pwq_s, pwq_d, 128, 16 * 2048)
    k_UT = cast_w("UT_s", UT_s, UT_d, NUV, 2048, parts=8)
    k_V = cast_w("V_s", V_s, V_d, NUV, 2048, parts=8)

    if DBG == "p0":
        P.barrier()
        for tt in range(16):
            P.dma("sp", out_d[tt * 128:(tt + 1) * 128, :], xtok_d[tt * 128:(tt + 1) * 128, :])
        P.emit()
        return nc
    rot = {"mm": 0, "sc": 0, "po": 0}

    def bank(kind):
        base = {"mm": 0, "sc": 2, "po": 4}[kind]
        i = base + (rot[kind] & 1)
        rot[kind] += 1
        return i

    xT_keys = [("xT", c) for c in range(16)]

    wblk = carve([128, 2, 3 * 2048], BF16)
    qT = carve([128, 2048], BF16)
    kT = carve([128, 2048], BF16)
    Vp = carve([128, 16, 130], BF16)
    PT = carve([128, 16, 512], BF16)
    kmf = carve([128, 8], F32)
    kmb = carve([128, 8], BF16)
    gbuf = carve([128, 8], F32)
    m8 = carve([128, 8], F32)
    selall = carve([128, 16, 8], F32)
    Oacc2 = [carve([128, 132], F32) for _ in range(2)]
    rden2 = [carve([128, 1], F32) for _ in range(2)]
    obf2 = [carve([128, 128], BF16) for _ in range(2)]
    mpend = {"n": 0, "p": None}

    def moba_flush():
        if mpend["p"] is None:
            return
        h_, qt_, pq_ = mpend["p"]
        mpend["p"] = None
        P.op("pe", "transpose", out=psb(7, BF16)[:, 0:128], in_=obf2[pq_][:], identity=identb[:],
             reads=[("obf", pq_), "identb"], writes=[("ps", 7)])
        P.op("act", "copy", out=B3[:, h_, qt_ * 128:(qt_ + 1) * 128], in_=psb(7, BF16)[:, 0:128],
             reads=[("ps", 7)], writes=[("actT", qt_)])
    win3 = win_s.rearrange("p (k n) -> p k n", n=2048)
    SCALE_M = 128.0 ** -0.5

    P.op("dve", "memset", ap=Vp[:, :, 128:130], constant=1.0, writes=["Vp_ones"])

    def proj_T(dst, wsl, key_dst, wkey, extra=None):
        w3 = wsl.rearrange("p (c n) -> p c n", n=128)
        for g in range(4):
            b = bank("mm")
            for c in range(16):
                P.op("pe", "matmul", out=psb(b)[:, 0:512], lhsT=w3[:, c, :],
                                                            rhs=A3[:, c, g * 512:(g + 1) * 512],
                                                            start=(c == 0), stop=(c == 15),
                     reads=[("xT", c), wkey], writes=[("ps", b)])
            P.op("act", "copy", out=dst[:, g * 512:(g + 1) * 512], in_=psb(b)[:, 0:512],
                 reads=[("ps", b)], writes=[(key_dst, g)])
            if extra is not None:
                extra(b, g)

    for h in range(8):
        wb = wblk[:, h & 1, :]
        src = win_s.rearrange("p (a b n) -> p a b n", a=6, b=8)[:, 0:3, h, :]
        wkey = ("wblk", h & 1)
        P.dma("sp", wb[:, 0:3 * 2048].rearrange("p (a n) -> p a n", n=2048), src, reads=k_win, writes=[wkey])
        if h == 0:
            ckpt("p1w")
        proj_T(qT, wb[:, 0:2048], "qT", wkey)
        if h == 0:
            ckpt("p1a")

        def km_extra(b, g):
            P.op("dve", "tensor_reduce", out=kmf[:, 2 * g:2 * g + 2], in_=psb(b)[:, 0:512].rearrange("p (a k) -> p a k", k=256),
                axis=AX.X, op=ALU.add, reads=[("ps", b)], writes=["kmf", ("ps", b)])

        proj_T(kT, wb[:, 2048:4096], "kT", wkey, extra=km_extra)
        if h == 0:
            ckpt("p1b")
        P.op("dve", "tensor_scalar", out=kmb[:], in0=kmf[:], scalar1=1.0 / 256.0, scalar2=None,
                                              op0=ALU.mult, reads=["kmf"], writes=["kmb"])
        wv3 = wb[:, 4096:6144].rearrange("p (c n) -> p c n", n=128)
        for k4 in range(4):
            b = bank("mm")
            for j in range(4):
                kt = k4 * 4 + j
                for c in range(16):
                    P.op("pe", "matmul", out=psb(b)[:, j * 128:(j + 1) * 128], lhsT=A3[:, c, kt * 128:(kt + 1) * 128], rhs=wv3[:, c, :],
                        start=(c == 0), stop=(c == 15), reads=[("xT", c), wkey], writes=[("ps", b)])
            P.op("act", "copy", out=Vp[:, k4 * 4:(k4 + 1) * 4, 0:128], in_=psb(b)[:, 0:512].rearrange("p (a n) -> p a n", n=128),
                 reads=[("ps", b)], writes=[("Vp", k4)])
        if h == 0:
            ckpt("p1")
        for qt in range(8, 16):
            qb = qt // 2
            P.op("pe", "matmul", out=psb(6)[:, 0:8], lhsT=qT[:, qt * 128:(qt + 1) * 128], rhs=kmb[:],
                                                 start=True, stop=True,
                 reads=[("qT", qt // 4), "kmb"], writes=[("ps", 6)])
            P.op("dve", "memset", ap=gbuf[:], constant=NEG, writes=["gbuf"])
            P.op("dve", "tensor_copy", out=gbuf[:, 0:qb], in_=psb(6)[:, 0:qb],
                 reads=[("ps", 6)], writes=["gbuf"])
            P.op("dve", "max", out=m8[:], in_=gbuf[:], reads=["gbuf"], writes=["m8"])
            P.op("dve", "tensor_scalar", out=selall[:, qt, :], in0=gbuf[:], scalar1=m8[:, 2:3],
                                                         scalar2=None, op0=ALU.is_ge,
                 reads=["gbuf", "m8"], writes=[("sel", qt)])
        if h == 0:
            ckpt("p1g")
        for g in range(4):
            nk = 4 * g + 4
            for kt in range(nk):
                lo = 0 if kt < 4 * g else (kt - 4 * g) * 128
                b = bank("sc")
                P.op("pe", "matmul", out=psb(b)[:, lo:512], lhsT=kT[:, kt * 128:(kt + 1) * 128], rhs=qT[:, g * 512 + lo:(g + 1) * 512],
                    start=True, stop=True, reads=[("kT", kt // 4), ("qT", g)], writes=[("ps", b)])
                P.op("act", "activation", out=PT[:, kt, lo:512], in_=psb(b)[:, lo:512],
                                                                      func=AF.Exp, scale=SCALE_M,
                     reads=[("ps", b)], writes=[("PT", kt)])
                if kt >= 4 * g:
                    P.op("dve", "tensor_tensor", out=PT[:, kt, lo:lo + 128],
                                                                        in0=PT[:, kt, lo:lo + 128], in1=triLb[:],
                                                                        op=ALU.mult,
                         reads=[("PT", kt), "triLb"], writes=[("PT", kt)])
            for qi in range(4):
                qt = 4 * g + qi
                qb = qt // 2
                pq = mpend["n"] & 1
                mpend["n"] += 1
                Oacc, rden, obf = Oacc2[pq], rden2[pq], obf2[pq]
                kO = ("Oacc", pq)
                for n in range(qb + 1):
                    kts = [k_ for k_ in (2 * n, 2 * n + 1) if k_ <= qt]
                    b = bank("po")
                    for ii, kt in enumerate(kts):
                        P.op("pe", "matmul", out=psb(b)[:, 0:130], lhsT=PT[:, kt, qi * 128:(qi + 1) * 128], rhs=Vp[:, kt, 0:130],
                             start=(ii == 0), stop=(ii == len(kts) - 1),
                             reads=[("PT", kt), ("Vp", kt // 4), "Vp_ones"], writes=[("ps", b)])
                    use_sel = (qb >= 4 and n < qb)
                    if n == 0:
                        if use_sel:
                            P.op("dve", "tensor_scalar", out=Oacc[:, 0:129], in0=psb(b)[:, 0:129], scalar1=selall[:, qt, n:n + 1],
                                 scalar2=None, op0=ALU.mult, reads=[("ps", b), ("sel", qt)], writes=[kO])
                        else:
                            P.op("dve", "tensor_copy", out=Oacc[:, 0:129], in_=psb(b)[:, 0:129],
                                 reads=[("ps", b)], writes=[kO])
                    else:
                        sc = selall[:, qt, n:n + 1] if use_sel else 1.0
                        P.op("dve", "scalar_tensor_tensor", out=Oacc[:, 0:129], in0=psb(b)[:, 0:129], scalar=sc, in1=Oacc[:, 0:129],
                             op0=ALU.mult, op1=ALU.add, reads=[("ps", b), ("sel", qt), kO], writes=[kO])
                P.op("dve", "reciprocal", out=rden[:], in_=Oacc[:, 128:129], reads=[kO], writes=[("rden", pq)])
                P.op("dve", "tensor_scalar", out=obf[:], in0=Oacc[:, 0:128], scalar1=rden[:, 0:1],
                     scalar2=None, op0=ALU.mult, reads=[kO, ("rden", pq)], writes=[("obf", pq)])
                moba_flush()
                mpend["p"] = (h, qt, pq)
    moba_flush()

    ckpt("p1m")
    phase_reset()
    wblk6 = carve([128, 6 * 2048], BF16)
    qT = carve([128, 2048], BF16)
    kT = carve([128, 2048], BF16)
    glT = carve([16, 2048], F32)
    gup = carve([16, 512], F32)
    gbias = carve([1, 512], F32)
    gnorm = carve([128, 256], F32)
    wglb = carve([128, 16, 16], BF16)
    stf = carve([128, 256], F32)
    stb = carve([128, 256], BF16)
    GT = []
    for _p in range(2):
        GT.append(dict(
            spb=carve([128, 128], F32), ksx=carve([128, 128], F32), kstate=carve([128, 128], BF16),
            Ep=carve([128, 128], F32), Em=carve([128, 128], F32), qd=carve([128, 128], BF16),
            kd=carve([128, 128], BF16), vbf=carve([128, 256], BF16), rs=carve([128, 256], F32),
            aTb=carve([128, 128], BF16), junk=carve([128, 256], F32), ss=carve([128, 2], F32),
            og=carve([128, 256], F32), ogb=carve([128, 256], BF16), etmp=carve([128, 128], F32)))

    P.dma("sp", gup[:], gup_d, writes=["gup"])
    P.dma("sp", gbias[:], gbias_d, writes=["gbias"])
    P.dma("sp", gnorm[:], gnorm_d, writes=["gnorm"])
    P.dma("pool", wglb[:].rearrange("p c n -> p (c n)"), wgl_d, writes=["wglb"])
    for g in range(4):
        b = bank("mm")
        for c in range(16):
            P.op("pe", "matmul", out=psb(b)[0:16, 0:512], lhsT=wglb[:, c, :],
                                                        rhs=A3[:, c, g * 512:(g + 1) * 512],
                                                        start=(c == 0), stop=(c == 15),
                 reads=[("xT", c), "wglb"], writes=[("ps", b)])
        P.op("act", "copy", out=glT[:, g * 512:(g + 1) * 512], in_=psb(b)[0:16, 0:512],
             reads=[("ps", b)], writes=["glT"])

    DK_SCALE = 128.0 ** -0.5
    for hg in range(4):
        wb = wblk6
        wkey = "wblk6"
        for i, blk in enumerate([24 + hg, 28 + hg, 32 + 2 * hg, 33 + 2 * hg, 40 + 2 * hg, 41 + 2 * hg]):
            P.dma("sp", wb[:, i * 2048:(i + 1) * 2048], win3[:, blk, :], reads=k_win, writes=[wkey])
        proj_T(qT, wb[:, 0:2048], "qT", wkey)
        proj_T(kT, wb[:, 2048:4096], "kT", wkey)
        wk3 = wb[:, 2048:4096].rearrange("p (c n) -> p c n", n=128)
        wv4 = wb[:, 4096:8192].rearrange("p (a c n) -> p a c n", a=2, n=128)
        wr4 = wb[:, 8192:12288].rearrange("p (a c n) -> p a c n", a=2, n=128)
        P.op("dve", "memset", ap=stf[:], constant=0.0, writes=["stf"])
        P.op("dve", "memset", ap=stb[:], constant=0.0, writes=["stb"])
        gst = {}

        def gla_Xa(tt):
            tsl = slice(tt * 128, (tt + 1) * 128)
            _g = GT[tt & 1]
            spb, ksx, kstate, Ep, Em, qd, kd, vbf, rs = (_g[k_] for k_ in "spb ksx kstate Ep Em qd kd vbf rs".split())
            aTb, junk, ss, og, ogb, etmp = (_g[k_] for k_ in "aTb junk ss og ogb etmp".split())
            pp = tt & 1
            bkv = bank("mm")
            wb6 = wb.rearrange("p (a c n) -> p a c n", a=6, n=128)
            for c in range(16):
                P.op("pe", "matmul", out=psb(bkv)[:, 0:384].rearrange("p (a n) -> p a n", n=128), lhsT=A3[:, c, tsl],
                     rhs=wb6[:, 1:4, c, :], start=(c == 0), stop=(c == 15),
                     reads=[("xT", c), wkey], writes=[("ps", bkv)])
            br = bank("mm")
            for c in range(16):
                P.op("pe", "matmul", out=psb(br)[:, 0:256].rearrange("p (a n) -> p a n", n=128), lhsT=A3[:, c, tsl],
                     rhs=wb6[:, 4:6, c, :], start=(c == 0), stop=(c == 15),
                     reads=[("xT", c), wkey], writes=[("ps", br)])
            bz = bank("sc")
            P.op("pe", "matmul", out=psb(bz)[:, 0:128], lhsT=glT[:, tsl],
                                                               rhs=gup[:, hg * 128:(hg + 1) * 128],
                                                               start=True, stop=False,
                 reads=["glT", "gup"], writes=[("ps", bz)])
            P.op("pe", "matmul", out=psb(bz)[:, 0:128], lhsT=onesf[0:1, :],
                                                      rhs=gbias[:, hg * 128:(hg + 1) * 128], start=False, stop=True,
                 reads=["onesf", "gbias"], writes=[("ps", bz)])
            P.op("act", "activation", out=etmp[:], in_=psb(bz)[:, 0:128], func=AF.Exp, scale=-1.0,
                 reads=[("ps", bz)], writes=[("etmp", pp)])
            P.op("act", "activation", out=spb[:], in_=etmp[:], func=AF.Ln, bias=1.0, scale=1.0,
                 reads=[("etmp", pp)], writes=[("spb", pp)])
            gst[tt] = (bkv, br, bz)

        def gla_Xb(tt):
            tsl = slice(tt * 128, (tt + 1) * 128)
            _g = GT[tt & 1]
            spb, ksx, kstate, Ep, Em, qd, kd, vbf, rs = (_g[k_] for k_ in "spb ksx kstate Ep Em qd kd vbf rs".split())
            aTb, junk, ss, og, ogb, etmp = (_g[k_] for k_ in "aTb junk ss og ogb etmp".split())
            pp = tt & 1
            bkv, br, bz = gst[tt]
            P.op("pe", "matmul", out=psb(bz)[:, 128:256], lhsT=triUf[:], rhs=spb[:], start=True, stop=True,
                 reads=["triUf", ("spb", pp)], writes=[("ps", bz)])
            P.op("pe", "matmul", out=psb(bz)[:, 256:384], lhsT=spb[:], rhs=triLf[:], start=True, stop=True,
                 reads=["triLf", ("spb", pp)], writes=[("ps", bz)])
            P.op("act", "activation", out=ksx[:], in_=psb(bz)[:, 128:256], func=AF.Exp, scale=-1.0 / 16.0,
                 reads=[("ps", bz)], writes=[("ksx", pp)])
            P.op("act", "activation", out=Ep[:], in_=psb(bz)[:, 256:384], func=AF.Exp, scale=-1.0 / 16.0,
                 reads=[("ps", bz)], writes=[("Ep", pp)])
            P.op("act", "activation", out=Em[:], in_=psb(bz)[:, 256:384], func=AF.Exp, scale=1.0 / 16.0,
                 reads=[("ps", bz)], writes=[("Em", pp)])
            P.op("dve", "tensor_tensor", out=kstate[:], in0=psb(bkv)[:, 0:128], in1=ksx[:], op=ALU.mult,
                 reads=[("ps", bkv), ("ksx", pp)], writes=[("kstate", pp)])
            P.op("dve", "scalar_tensor_tensor", out=qd[:], in0=qT[:, tsl], scalar=DK_SCALE, in1=Ep[:],
                                                                  op0=ALU.mult, op1=ALU.mult,
                 reads=[("qT", tt // 4), ("Ep", pp)], writes=[("qd", pp)])
            P.op("dve", "tensor_tensor", out=kd[:], in0=kT[:, tsl], in1=Em[:], op=ALU.mult,
                 reads=[("kT", tt // 4), ("Em", pp)], writes=[("kd", pp)])
            P.op("act", "copy", out=vbf[:], in_=psb(bkv)[:, 128:384], reads=[("ps", bkv)], writes=[("vbf", pp), ("ps", bkv)])
            P.op("act", "activation", out=rs[:], in_=psb(br)[:, 0:256], func=AF.Silu,
                 reads=[("ps", br)], writes=[("rs", pp)])


        def gla_Ya(tt):
            tsl = slice(tt * 128, (tt + 1) * 128)
            _g = GT[tt & 1]
            spb, ksx, kstate, Ep, Em, qd, kd, vbf, rs = (_g[k_] for k_ in "spb ksx kstate Ep Em qd kd vbf rs".split())
            aTb, junk, ss, og, ogb, etmp = (_g[k_] for k_ in "aTb junk ss og ogb etmp".split())
            pp = tt & 1
            ba = bank("po")
            P.op("pe", "matmul", out=psb(ba)[:, 0:128], lhsT=kd[:], rhs=qd[:], start=True, stop=True,
                 reads=[("kd", pp), ("qd", pp)], writes=[("ps", ba)])
            P.op("dve", "tensor_tensor", out=aTb[:], in0=psb(ba)[:, 0:128], in1=triLf[:], op=ALU.mult,
                 reads=[("ps", ba), "triLf"], writes=[("aTb", pp)])

        def gla_Yb(tt):
            tsl = slice(tt * 128, (tt + 1) * 128)
            _g = GT[tt & 1]
            spb, ksx, kstate, Ep, Em, qd, kd, vbf, rs = (_g[k_] for k_ in "spb ksx kstate Ep Em qd kd vbf rs".split())
            aTb, junk, ss, og, ogb, etmp = (_g[k_] for k_ in "aTb junk ss og ogb etmp".split())
            pp = tt & 1
            bo = bank("po")
            P.op("pe", "matmul", out=psb(bo)[:, 0:256], lhsT=aTb[:], rhs=vbf[:], start=True, stop=False,
                 reads=[("aTb", pp), ("vbf", pp)], writes=[("ps", bo)])
            P.op("pe", "matmul", out=psb(bo)[:, 0:256], lhsT=qd[:], rhs=stb[:], start=False, stop=True,
                 reads=[("qd", pp), "stb"], writes=[("ps", bo)])
            P.op("pe", "matmul", out=psb(bo)[:, 256:512], lhsT=kstate[:], rhs=vbf[:], start=True, stop=True,
                 reads=[("kstate", pp), ("vbf", pp)], writes=[("ps", bo)])
            P.op("dve", "scalar_tensor_tensor", out=stf[:], in0=stf[:], scalar=Ep[:, 127:128],
                                                               in1=psb(bo)[:, 256:512], op0=ALU.mult, op1=ALU.add,
                 reads=[("ps", bo), ("Ep", pp), "stf"], writes=["stf"])
            P.op("dve", "tensor_copy", out=stb[:], in_=stf[:], reads=["stf"], writes=["stb"])
            P.op("act", "activation", out=junk[:], in_=psb(bo)[:, 0:256], func=AF.Square,
                 reads=[("ps", bo)], writes=[("junk", pp), ("ps", bo)])
            P.op("dve", "tensor_reduce", out=ss[:, 0:1], in_=junk[:], axis=AX.X, op=ALU.add, reads=[("junk", pp)], writes=[("ss", pp)])
            P.op("dve", "tensor_scalar", out=ss[:, 1:2], in0=ss[:, 0:1], scalar1=1.0 / 256.0, scalar2=1e-6,
                                                  op0=ALU.mult, op1=ALU.add, reads=[("ss", pp)], writes=[("ss1", pp)])
            P.op("act", "activation", out=ss[:, 1:2], in_=ss[:, 1:2], func=AF.Sqrt, reads=[("ss1", pp)], writes=[("ss1", pp)])
            P.op("dve", "reciprocal", out=ss[:, 1:2], in_=ss[:, 1:2], reads=[("ss1", pp)], writes=[("ss1", pp)])
            P.op("dve", "scalar_tensor_tensor", out=og[:], in0=psb(bo)[:, 0:256], scalar=ss[:, 1:2],
                                                               in1=gnorm[:], op0=ALU.mult, op1=ALU.mult,
                 reads=[("ps", bo), ("ss1", pp), "gnorm"], writes=[("og", pp)])
            P.op("dve", "tensor_tensor", out=ogb[:], in0=og[:], in1=rs[:], op=ALU.mult,
                 reads=[("og", pp), ("rs", pp)], writes=[("ogb", pp)])

        def gla_Yc(tt):
            tsl = slice(tt * 128, (tt + 1) * 128)
            _g = GT[tt & 1]
            spb, ksx, kstate, Ep, Em, qd, kd, vbf, rs = (_g[k_] for k_ in "spb ksx kstate Ep Em qd kd vbf rs".split())
            aTb, junk, ss, og, ogb, etmp = (_g[k_] for k_ in "aTb junk ss og ogb etmp".split())
            pp = tt & 1
            for a in range(2):
                P.op("pe", "transpose", out=psb(7, BF16)[:, a * 128:(a + 1) * 128],
                                                     in_=ogb[:, a * 128:(a + 1) * 128], identity=identb[:],
                     reads=[("ogb", pp), "identb"], writes=[("ps", 7)])
            P.op("act", "copy", out=B3[:, 8 + 2 * hg:10 + 2 * hg, tsl], in_=psb(7, BF16)[:, 0:256].rearrange("p (a n) -> p a n", n=128),
                 reads=[("ps", 7)], writes=[("actT", tt)])


        for step in range(18):
            if step < 16:
                gla_Xa(step)
            if 1 <= step <= 16:
                gla_Ya(step - 1)
            if step < 16:
                gla_Xb(step)
            if 1 <= step <= 16:
                gla_Yb(step - 1)
            if step >= 2:
                gla_Yc(step - 2)

    ckpt("p1gla")
    LN = {}

    def ln_bufs(nbuf=2):
        LN["n"] = nbuf
        LN["LNP"] = carve([128, 2, 2048], F32)
        LN["RES"] = [carve([128, 2048], F32) for _ in range(nbuf)]
        LN["stats"] = [carve([128, 4, 6], F32) for _ in range(nbuf)]
        LN["mv"] = [carve([128, 4], F32) for _ in range(nbuf)]

    def ln_phase(w_scr, w_keys, res_src, res_key, lnidx, dst_scr, dst_key):
        phase_reset()
        ln_bufs(3)
        LN["ybf"] = [carve([128, 2048], BF16) for _ in range(3)]
        LNP = LN["LNP"]
        P.dma("sp", BUFA[:], w_scr, reads=w_keys, writes=["BUFA"])
        P.dma("sp", LNP[:, 0, :], lnp_d[2 * lnidx], writes=["LNP"])
        P.dma("sp", LNP[:, 1, :], lnp_d[2 * lnidx + 1], writes=["LNP"])
        def res_load(t_):
            if t_ < 16:
                p_ = t_ % LN["n"]
                P.dma("sp", LN["RES"][p_][:], res_src[t_ * 128:(t_ + 1) * 128, :],
                      reads=([(res_key, t_)] if res_key else []), writes=[("RES", p_)])

        res_load(0)
        res_load(1)
        for tt in range(16):
            tsl = slice(tt * 128, (tt + 1) * 128)
            p = tt % LN["n"]
            bs = []
            for ds in range(4):
                b = ds
                bs.append(b)
                for c in range(16):
                    P.op("pe", "matmul", out=psb(b)[:, 0:512], lhsT=B3[:, c, tsl], rhs=A3[:, c, ds * 512:(ds + 1) * 512],
                         start=(c == 0), stop=(c == 15), reads=[("actT", tt), "BUFA"], writes=[("ps", b)])
            if tt > 0:
                ln_tail_b(tt - 1)
            ln_tail(tt, tsl, bs, dst_scr, dst_key, True)
            res_load(tt + 2)
        ln_tail_b(15)

    def ln_tail_b(tt):
        p = tt % LN["n"]
        tsl = slice(tt * 128, (tt + 1) * 128)
        ybf = LN["ybf"][p]
        for c4 in range(4):
            b = 4 + (c4 & 1)
            for j in range(4):
                c = c4 * 4 + j
                P.op("pe", "transpose", out=psb(b, BF16)[:, j * 128:(j + 1) * 128],
                     in_=ybf[:, c * 128:(c + 1) * 128], identity=identb[:],
                     reads=[("ybf", p), "identb"], writes=[("ps", b)])
            P.op("act", "copy", out=B3[:, c4 * 4:(c4 + 1) * 4, tsl],
                 in_=psb(b, BF16)[:, 0:512].rearrange("p (a n) -> p a n", n=128),
                 reads=[("ps", b)], writes=[("actT", tt)])

    def ln_tail(tt, tsl, bs, dst_scr, dst_key, final):
        p = tt % LN["n"]
        LNP, RES, stats, mv = LN["LNP"], LN["RES"][p], LN["stats"][p], LN["mv"][p]
        kR, kS, kM = ("RES", p), ("stats", p), ("mv", p)
        for ds, b in enumerate(bs):
            dsl = slice(ds * 512, (ds + 1) * 512)
            P.op("dve", "scalar_tensor_tensor", out=RES[:, dsl], in0=RES[:, dsl], scalar=ALPHA, in1=psb(b)[:, 0:512],
                 op0=ALU.mult, op1=ALU.add, reads=[("ps", b), kR], writes=[kR])
        for ds, b in enumerate(bs):
            dsl = slice(ds * 512, (ds + 1) * 512)
            P.op("dve", "bn_stats", out=stats[:, ds, :], in_=RES[:, dsl], reads=[kR], writes=[kS])
        P.op("dve", "bn_aggr", out=mv[:, 0:2], in_=stats[:].rearrange("p a b -> p (a b)"),
             reads=[kS], writes=[kM])
        P.op("dve", "tensor_scalar", out=mv[:, 2:3], in0=mv[:, 1:2], scalar1=1e-5, scalar2=None, op0=ALU.add,
             reads=[kM], writes=[kM])
        P.op("act", "activation", out=mv[:, 2:3], in_=mv[:, 2:3], func=AF.Sqrt, reads=[kM], writes=[kM])
        P.op("dve", "reciprocal", out=mv[:, 3:4], in_=mv[:, 2:3], reads=[kM], writes=[kM])
        P.op("dve", "tensor_scalar", out=RES[:], in0=RES[:], scalar1=mv[:, 0:1], scalar2=mv[:, 3:4],
             op0=ALU.subtract, op1=ALU.mult, reads=[kR, kM], writes=[kR])
        P.op("dve", "tensor_tensor", out=RES[:], in0=RES[:], in1=LNP[:, 0, :], op=ALU.mult,
             reads=[kR, "LNP"], writes=[kR])
        P.op("dve", "tensor_tensor", out=RES[:], in0=RES[:], in1=LNP[:, 1, :], op=ALU.add,
             reads=[kR, "LNP"], writes=[kR])
        P.dma("sp", dst_scr[tsl, :], RES[:], reads=[kR], writes=[(dst_key, tt)])
        if "ybf" not in LN or LN["ybf"] is None:
            return
        ybf = LN["ybf"][p]
        P.op("act", "copy", out=ybf[:], in_=RES[:], reads=[kR], writes=[("ybf", p)])
        return
        for c4 in range(4):
            b = 4 + (c4 & 1)
            for j in range(4):
                c = c4 * 4 + j
                P.op("pe", "transpose", out=psb(b, BF16)[:, j * 128:(j + 1) * 128],
                     in_=ybf[:, c * 128:(c + 1) * 128], identity=identb[:],
                     reads=[("ybf", p), "identb"], writes=[("ps", b)])
            P.op("act", "copy", out=B3[:, c4 * 4:(c4 + 1) * 4, tsl],
                 in_=psb(b, BF16)[:, 0:512].rearrange("p (a n) -> p a n", n=128),
                 reads=[("ps", b)], writes=[("actT", tt)])

    ln_phase(wout_s, k_wout, xtok_d, None, 0, out_d if DBG == "h1" else h1_s, "h1")
    if DBG == "h1":
        P.emit()
        return nc

    phase_reset()
    memTb = carve([128, 16, 256], BF16)
    KmT = carve([128, 16, 256], BF16)
    Vm = carve([128, 2, 2048], BF16)
    wkvb = carve([128, 2, 16 * 256], BF16)
    qg = carve([128, 16, 512], BF16)
    PTx = carve([128, 2, 512], BF16)
    xo2 = [carve([128, 512], BF16) for _ in range(2)]
    rdx2 = [carve([128, 2], F32) for _ in range(2)]
    xpend = {"n": 0, "p": None}

    def x_flush():
        if xpend["p"] is None:
            return
        h_, tt_, px_ = xpend["p"]
        xpend["p"] = None
        for j in range(4):
            P.op("pe", "transpose", out=psb(7, BF16)[:, j * 128:(j + 1) * 128],
                 in_=xo2[px_][:, j * 128:(j + 1) * 128], identity=identb[:],
                 reads=[("xo", px_), "identb"], writes=[("ps", 7)])
        P.op("act", "copy", out=B3[:, 4 * h_:4 * h_ + 4, tt_ * 128:(tt_ + 1) * 128],
             in_=psb(7, BF16)[:, 0:512].rearrange("p (a n) -> p a n", n=128),
             reads=[("ps", 7)], writes=[("actT", tt_)])
    for c in range(16):
        P.dma("pool", memTb[:, c, :], memT_d[c * 128:(c + 1) * 128, :], writes=["memTb"])
    wkv3 = wkv_s.rearrange("p (k n) -> p k n", n=16 * 256)
    for blk in range(16):
        wv_ = wkvb[:, blk & 1, :]
        w3 = wv_.rearrange("p (c n) -> p c n", n=256)
        P.dma("sp", wv_, wkv3[:, blk, :], reads=k_wkv, writes=[("wkvb", blk & 1)])
        if blk < 8:
            for j in range(2):
                cc = blk * 2 + j
                b = bank("mm")
                for c in range(16):
                    P.op("pe", "matmul", out=psb(b)[:, 0:256], lhsT=w3[:, c, j * 128:(j + 1) * 128], rhs=memTb[:, c, :],
                         start=(c == 0), stop=(c == 15), reads=["memTb", ("wkvb", blk & 1)], writes=[("ps", b)])
                P.op("act", "copy", out=KmT[:, cc, :], in_=psb(b)[:, 0:256], reads=[("ps", b)], writes=["KmT"])
        else:
            ds = blk - 8
            for mt in range(2):
                b = bank("mm")
                for c in range(16):
                    P.op("pe", "matmul", out=psb(b)[:, 0:256], lhsT=memTb[:, c, mt * 128:(mt + 1) * 128], rhs=w3[:, c, :],
                         start=(c == 0), stop=(c == 15), reads=["memTb", ("wkvb", blk & 1)], writes=[("ps", b)])
                P.op("act", "copy", out=Vm[:, mt, ds * 256:(ds + 1) * 256], in_=psb(b)[:, 0:256],
                     reads=[("ps", b)], writes=["Vm"])
    ckpt("p3kv")
    P.dma("sp", BUFA[:], wq_s, reads=k_wq, writes=["BUFA"])
    SCALE_X = 512.0 ** -0.5
    for g in range(4):
        gsl = slice(g * 512, (g + 1) * 512)
        gkeys = [("actT", 4 * g + i) for i in range(4)]
        for cc in range(16):
            b = bank("mm")
            for c in range(16):
                P.op("pe", "matmul", out=psb(b)[:, 0:512], lhsT=A3[:, c, cc * 128:(cc + 1) * 128], rhs=B3[:, c, gsl],
                    start=(c == 0), stop=(c == 15), reads=gkeys + ["BUFA"], writes=[("ps", b)])
            P.op("act", "copy", out=qg[:, cc, :], in_=psb(b)[:, 0:512],
                 reads=[("ps", b)], writes=[("qg", cc // 4)])
        for h in range(4):
            for mt in range(2):
                b = bank("sc")
                for j in range(4):
                    P.op("pe", "matmul", out=psb(b)[:, 0:512], lhsT=KmT[:, 4 * h + j, mt * 128:(mt + 1) * 128], rhs=qg[:, 4 * h + j, :],
                        start=(j == 0), stop=(j == 3), reads=["KmT", ("qg", h)], writes=[("ps", b)])
                P.op("act", "activation", out=PTx[:, mt, :], in_=psb(b)[:, 0:512], func=AF.Exp,
                                                               scale=SCALE_X, reads=[("ps", b)], writes=[("PTx", mt)])
            for ti in range(4):
                tt = 4 * g + ti
                b = bank("po")
                for mt in range(2):
                    P.op("pe", "matmul", out=psb(b)[:, 0:512], lhsT=PTx[:, mt, ti * 128:(ti + 1) * 128], rhs=Vm[:, mt, h * 512:(h + 1) * 512],
                        start=(mt == 0), stop=(mt == 1), reads=[("PTx", mt), "Vm"], writes=[("ps", b)])
                for mt in range(2):
                    P.op("pe", "matmul", out=psb(6)[:, 0:2], lhsT=PTx[:, mt, ti * 128:(ti + 1) * 128], rhs=onesb[:, 0:2],
                        start=(mt == 0), stop=(mt == 1), reads=[("PTx", mt), "onesb"], writes=[("ps", 6)])
                px = xpend["n"] & 1
                xpend["n"] += 1
                P.op("dve", "reciprocal", out=rdx2[px][:, 0:1], in_=psb(6)[:, 0:1], reads=[("ps", 6)], writes=[("rdx", px)])
                P.op("dve", "tensor_scalar", out=xo2[px][:], in0=psb(b)[:, 0:512], scalar1=rdx2[px][:, 0:1],
                     scalar2=None, op0=ALU.mult, reads=[("ps", b), ("rdx", px)], writes=[("xo", px)])
                x_flush()
                xpend["p"] = (h, tt, px)
    x_flush()

    ckpt("p3a")
    ln_phase(wo_s, k_wo, h1_s, "h1", 1, out_d if DBG == "h2" else h2_s, "h2")
    if DBG == "h2":
        P.emit()
        return nc

    phase_reset()
    IJ = carve([128, 16, 2, 128], BF16)
    GG = carve([128, 16, 128], F32)
    off_keep = wk["off"]
    skTb = carve([128, 16, 128], BF16)
    qpT = carve([128, 16, 128], BF16)
    sbuf_s = carve([128, 16, 128], F32)
    s2 = carve([128, 16, 128], F32)
    vals = carve([128, 16, 16], F32)
    idxu = carve([128, 16, 16], U32)
    idxf2 = [carve([128, 16, 16], F32) for _ in range(2)]
    cand = carve([128, 8, 256], F32)
    c2 = carve([128, 8, 256], F32)
    best2 = [carve([128, 8, 16], F32) for _ in range(2)]
    posu2 = [carve([128, 8, 16], U32) for _ in range(2)]
    gt0 = carve([128, 8, 16], F32)
    gt1 = carve([128, 8, 16], F32)
    posf = carve([128, 8, 16], F32)
    thr16 = carve([128, 16], F32)
    pa_f = carve([128, 8, 16], F32)
    pb_f = carve([128, 8, 16], F32)
    oh = carve([128, 8, 16, 16], F32)
    zz = carve([128, 8], F32)
    P.op("dve", "tensor_scalar", out=thr16[:], in0=iotaf[:, 0:16], scalar1=1.0, scalar2=16.0, op0=ALU.add, op1=ALU.mult,
         reads=["iotaf"], writes=["thr16"])
    P.dma("pool", skTb[:].rearrange("p c n -> p (c n)"), skT_d, writes=["skTb"])
    P.dma("sp", BUFA[:], pwq_s, reads=k_pwq, writes=["BUFA"])
    io16 = iotaf[:, 0:16]
    prev_tail = []
    for tt in range(16):
        tsl = slice(tt * 128, (tt + 1) * 128)
        for c4 in range(4):
            b = bank("mm")
            for j in range(4):
                cc = c4 * 4 + j
                for c in range(16):
                    P.op("pe", "matmul", out=psb(b)[:, j * 128:(j + 1) * 128], lhsT=A3[:, c, cc * 128:(cc + 1) * 128], rhs=B3[:, c, tsl],
                        start=(c == 0), stop=(c == 15), reads=[("actT", tt), "BUFA"], writes=[("ps", b)])
            P.op("act", "copy", out=qpT[:, c4 * 4:(c4 + 1) * 4, :],
                                                     in_=psb(b)[:, 0:512].rearrange("p (a n) -> p a n", n=128),
                 reads=[("ps", b)], writes=[("qpT", c4)])
        for c4 in range(4):
            b = bank("sc")
            for j in range(4):
                cc = c4 * 4 + j
                P.op("pe", "matmul", out=psb(b)[:, j * 128:(j + 1) * 128], lhsT=qpT[:, cc, :],
                                                              rhs=skTb[:, cc, :], start=True, stop=True,
                     reads=[("qpT", c4), "skTb"], writes=[("ps", b)])
            P.op("act", "copy", out=sbuf_s[:, c4 * 4:(c4 + 1) * 4, :],
                                                     in_=psb(b)[:, 0:512].rearrange("p (a n) -> p a n", n=128),
                 reads=[("ps", b)], writes=[("s", c4)])
        pt = tt & 1
        idxf, best, posu = idxf2[pt], best2[pt], posu2[pt]
        P.capture = []
        for cc in range(16):
            P.op("dve", "max", out=vals[:, cc, 0:8], in_=sbuf_s[:, cc, :], reads=[("s", cc // 4)], writes=[("vals", cc, 0)])
        for cc in range(16):
            P.op("dve", "match_replace", out=s2[:, cc, :], in_to_replace=vals[:, cc, 0:8], in_values=sbuf_s[:, cc, :],
                 imm_value=NEG, reads=[("s", cc // 4), ("vals", cc, 0)], writes=[("s2", cc)])
        for cc in range(16):
            P.op("dve", "max", out=vals[:, cc, 8:16], in_=s2[:, cc, :], reads=[("s2", cc)], writes=[("vals", cc, 1)])
        for cc in range(16):
            P.op("dve", "max_index", out=idxu[:, cc, 0:8], in_max=vals[:, cc, 0:8], in_values=sbuf_s[:, cc, :],
                 reads=[("s", cc // 4), ("vals", cc, 0)], writes=[("idxu", cc, 0)])
        for cc in range(16):
            P.op("dve", "max_index", out=idxu[:, cc, 8:16], in_max=vals[:, cc, 8:16], in_values=sbuf_s[:, cc, :],
                 reads=[("s", cc // 4), ("vals", cc, 1)], writes=[("idxu", cc, 1)])
        k_vals = [("vals", cc, hh) for cc in range(16) for hh in range(2)]
        k_idxu = [("idxu", cc, hh) for cc in range(16) for hh in range(2)]
        P.op("dve", "tensor_copy", out=idxf[:], in_=idxu[:], reads=k_idxu, writes=[("idxf", pt)])
        v4 = vals[:].rearrange("p (h two) k -> p h two k", two=2)
        i4 = idxf[:].rearrange("p (h two) k -> p h two k", two=2)
        cand4 = cand[:].rearrange("p h (a b) -> p h a b", b=16)
        P.op("dve", "tensor_tensor", out=cand4, in0=v4[:, :, 0, :].unsqueeze(3).to_broadcast([128, 8, 16, 16]),
             in1=v4[:, :, 1, :].unsqueeze(2).to_broadcast([128, 8, 16, 16]), op=ALU.add, reads=k_vals, writes=["cand"])
        for h in range(8):
            P.op("dve", "max", out=best[:, h, 0:8], in_=cand[:, h, :], reads=["cand"], writes=[("best", pt, h, 0)])
        for h in range(8):
            P.op("dve", "match_replace", out=c2[:, h, :], in_to_replace=best[:, h, 0:8], in_values=cand[:, h, :],
                 imm_value=NEG, reads=["cand", ("best", pt, h, 0)], writes=[("c2", h)])
        for h in range(8):
            P.op("dve", "max", out=best[:, h, 8:16], in_=c2[:, h, :], reads=[("c2", h)], writes=[("best", pt, h, 1)])
        for h in range(8):
            P.op("dve", "max_index", out=posu[:, h, 0:8], in_max=best[:, h, 0:8], in_values=cand[:, h, :],
                 reads=["cand", ("best", pt, h, 0)], writes=[("posu", pt, h, 0)])
        for h in range(8):
            P.op("dve", "max_index", out=posu[:, h, 8:16], in_max=best[:, h, 8:16], in_values=cand[:, h, :],
                 reads=["cand", ("best", pt, h, 1)], writes=[("posu", pt, h, 1)])
        k_best = [("best", pt, h, hh) for h in range(8) for hh in range(2)]
        k_posu = [("posu", pt, h, hh) for h in range(8) for hh in range(2)]
        capA = P.capture
        P.capture = []
        gsl_ = GG[:, tt, :].rearrange("p (h k) -> p h k", k=16)
        P.op("dve", "tensor_tensor", out=gt0[:], in0=best[:], in1=best[:, :, 0:1].to_broadcast([128, 8, 16]), op=ALU.subtract,
             reads=k_best, writes=["gt0"])
        P.op("act", "activation", out=gt1[:], in_=gt0[:], func=AF.Exp, reads=["gt0"], writes=["gt1"])
        P.op("dve", "tensor_reduce", out=zz[:], in_=gt1[:], axis=AX.X, op=ALU.add, reads=["gt1"], writes=["zz"])
        P.op("dve", "reciprocal", out=zz[:], in_=zz[:], reads=["zz"], writes=["zz"])
        P.op("dve", "tensor_tensor", out=gsl_, in0=gt1[:], in1=zz[:].unsqueeze(2).to_broadcast([128, 8, 16]), op=ALU.mult,
             reads=["gt1", "zz"], writes=[("GG", tt)])
        P.op("dve", "tensor_copy", out=posf[:], in_=posu[:], reads=k_posu, writes=["posf"])
        P.op("dve", "tensor_tensor", out=oh[:], in0=posf[:].unsqueeze(3).to_broadcast([128, 8, 16, 16]),
             in1=thr16[:].unsqueeze(1).unsqueeze(1).to_broadcast([128, 8, 16, 16]), op=ALU.is_ge,
             reads=["posf", "thr16"], writes=["oh0", "oh1", "oh"])
        P.op("dve", "tensor_reduce", out=pa_f[:], in_=oh[:], axis=AX.X, op=ALU.add, reads=["oh"], writes=["pa_f"])
        P.op("dve", "scalar_tensor_tensor", out=pb_f[:], in0=pa_f[:], scalar=-16.0, in1=posf[:], op0=ALU.mult, op1=ALU.add,
             reads=["pa_f", "posf"], writes=["pb_f"])
        for which, pf in ((0, pa_f), (1, pb_f)):
            P.op("dve", "tensor_tensor", out=oh[:], in0=pf[:].unsqueeze(3).to_broadcast([128, 8, 16, 16]),
                in1=io16.unsqueeze(1).unsqueeze(1).to_broadcast([128, 8, 16, 16]), op=ALU.is_equal,
                 reads=["pa_f", "pb_f", "iotaf"], writes=["oh0", "oh1", "oh"])
            P.op("dve", "tensor_tensor", out=oh[:], in0=oh[:], in1=i4[:, :, which, :].unsqueeze(2).to_broadcast([128, 8, 16, 16]), op=ALU.mult,
                 reads=["oh", ("idxf", pt)], writes=["oh"])
            P.op("dve", "tensor_reduce", out=IJ[:, tt, which, :].rearrange("p (h k) -> p h k", k=16), in_=oh[:], axis=AX.X, op=ALU.add,
                 reads=["oh"], writes=[("IJ", tt)])
        capB = P.capture
        P.capture = None
        g_ops, i_ops = capB[:5], capB[5:]
        tail = []
        for k_ in range(max(len(g_ops), len(i_ops))):
            if k_ < len(i_ops):
                tail.append(i_ops[k_])
            if k_ < len(g_ops):
                tail.append(g_ops[k_])
        P.replay(capA, prev_tail)
        prev_tail = tail
    P.replay(prev_tail)

    ckpt("p4a")
    P.barrier()
    wk["off"] = off_keep
    ln_bufs(1)
    LN["ybf"] = None
    LNP, RES = LN["LNP"], LN["RES"][0]
    ItJt = carve([128, 2, 256], F32)
    gtT = carve([128, 256], F32)
    OIg = carve([128, 16, 128], BF16)
    OJ = carve([128, 2, 16 * 128], BF16)
    NRING = 4
    wring = carve([128, NRING, 2048], BF16)
    gel = carve([128, 2, 256], F32)
    IJf = carve([128, 2, 128], F32)
    ring = {"n": 0}
    P.dma("sp", LNP[:, 0, :], lnp_d[4], writes=["LNP"])
    P.dma("sp", LNP[:, 1, :], lnp_d[5], writes=["LNP"])
    G3 = BUFA[:].rearrange("p (j t) -> p j t", t=256)
    Gkeys = [("G", j) for j in range(128)]
    UT3 = UT_s.rearrange("(j p) n -> j p n", p=128)
    V3 = V_s.rearrange("(j p) n -> j p n", p=128)
    for grp in range(8):
        for u in range(2):
            tt = 2 * grp + u
            P.op("dve", "tensor_copy", out=IJf[:], in_=IJ[:, tt, :, :], reads=[("IJ", tt)], writes=["IJf"])
            for w_ in range(2):
                P.op("pe", "transpose", out=psb(6)[:, w_ * 128:(w_ + 1) * 128], in_=IJf[:, w_, :],
                                                       identity=identf[:], reads=["IJf", "identf"], writes=[("ps", 6)])
            P.op("pe", "transpose", out=psb(6)[:, 256:384], in_=GG[:, tt, :], identity=identf[:],
                 reads=[("GG", tt), "identf"], writes=[("ps", 6)])
            P.op("act", "copy", out=ItJt[:, :, u * 128:(u + 1) * 128],
                                              in_=psb(6)[:, 0:256].rearrange("p (a n) -> p a n", n=128),
                 reads=[("ps", 6)], writes=["ItJt"])
            P.op("act", "copy", out=gtT[:, u * 128:(u + 1) * 128], in_=psb(6)[:, 256:384],
                 reads=[("ps", 6)], writes=["gtT"])
        for t16 in range(16):
            t0 = t16 * 16
            pj = t16 & 1
            OJ3 = OJ[:, pj, :].rearrange("p (t j) -> p t j", j=128)
            P.op("dve", "tensor_tensor", out=OJ3, in0=iotaf[:].unsqueeze(1).to_broadcast([128, 16, 128]),
                 in1=ItJt[:, 1, t0:t0 + 16].unsqueeze(2).to_broadcast([128, 16, 128]), op=ALU.is_equal,
                 reads=["ItJt", "iotaf"], writes=[("OJ", pj)])
            for tl in range(16):
                P.op("dve", "tensor_scalar", out=OIg[:, tl, :], in0=iotaf[:], scalar1=ItJt[:, 0, t0 + tl:t0 + tl + 1],
                     scalar2=gtT[:, t0 + tl:t0 + tl + 1], op0=ALU.is_equal, op1=ALU.mult,
                     reads=["ItJt", "gtT", "iotaf"], writes=[("OIg", tl)])
            bkeys = [("ps", pj * 4 + q4) for q4 in range(4)]
            for tl in range(16):
                b = pj * 4 + tl // 4
                P.op("pe", "matmul", out=psb(b)[:, (tl % 4) * 128:(tl % 4 + 1) * 128], lhsT=OIg[:, tl, :],
                     rhs=OJ3[:, tl, :], start=True, stop=True,
                     reads=[("OIg", tl), ("OJ", pj)], writes=[("ps", b)])
            P.op("act", "copy", out=G3[:, :, t0:t0 + 16],
                 in_=PSALL[:, pj * 2048:(pj + 1) * 2048].rearrange("p (t j) -> p j t", j=128),
                 reads=bkeys, writes=Gkeys)
        if grp == 0:
            ckpt("p4g")
        gsl = slice(grp * 256, (grp + 1) * 256)
        akeys = [("actT", 2 * grp), ("actT", 2 * grp + 1)]
        for j in range(128):
            slot = ring["n"] % NRING
            ring["n"] += 1
            ub = wring[:, slot, :]
            u3 = ub.rearrange("p (c n) -> p c n", n=128)
            P.dma("sp", ub, UT3[j], reads=[k_UT[j // 16]], writes=[("wring", slot)])
            b = bank("sc")
            for c in range(16):
                P.op("pe", "matmul", out=psb(b)[:, 0:256], lhsT=u3[:, c, :], rhs=B3[:, c, gsl],
                                                                        start=(c == 0), stop=(c == 15),
                     reads=akeys + [("wring", slot)], writes=[("ps", b)])
            P.op("act", "activation", out=gel[:, j & 1, :], in_=psb(b)[:, 0:256], func=AF.Gelu,
                 reads=[("ps", b)], writes=[("gel", j & 1)])
            P.op("dve", "tensor_tensor", out=G3[:, j, :], in0=G3[:, j, :], in1=gel[:, j & 1, :], op=ALU.mult,
                 reads=[("gel", j & 1), ("G", j)], writes=[("G", j)])
        if grp == 0:
            ckpt("p4A")
        for j in range(128):
            slot = ring["n"] % NRING
            ring["n"] += 1
            vb = wring[:, slot, :]
            P.dma("sp", vb, V3[j], reads=[k_V[j // 16]], writes=[("wring", slot)])
            for u in range(2):
                for ds in range(4):
                    b = u * 4 + ds
                    P.op("pe", "matmul", out=psb(b)[:, 0:512], lhsT=G3[:, j, u * 128:(u + 1) * 128], rhs=vb[:, ds * 512:(ds + 1) * 512],
                        start=(j == 0), stop=(j == 127), reads=[("G", j), ("wring", slot)], writes=[("ps", b)])
        for u in range(2):
            tt = 2 * grp + u
            tsl = slice(tt * 128, (tt + 1) * 128)
            P.dma("sp", RES[:], h2_s[tsl, :], reads=[("h2", tt)], writes=[("RES", 0)])
            ln_tail(tt, tsl, [u * 4 + ds for ds in range(4)], out_d, "out", True)
    P.emit()
    return nc


def _prep_inputs(inp):
    f = np.float32
    w_in = np.asarray(inp["w_in"][0], f)
    com = {}
    com["w_in"] = np.ascontiguousarray(
        w_in[:, :6144].reshape(16, 128, 48, 128).transpose(1, 2, 0, 3)).reshape(128, 48 * 2048)
    com["w_gl"] = np.ascontiguousarray(w_in[:, 6144:].reshape(16, 128, 16).transpose(1, 0, 2)).reshape(128, 256)
    com["gate_up"] = np.ascontiguousarray(inp["gla_gate_up"][0], f)
    com["gate_bias"] = np.ascontiguousarray(inp["gla_gate_bias"][0], f).reshape(1, 512)
    com["gnorm"] = np.ascontiguousarray(np.broadcast_to(np.asarray(inp["gla_norm_g"][0], f)[None, :], (128, 256)))

    def kmaj(w):
        w = np.asarray(w, f)
        return np.ascontiguousarray(w.reshape(16, 128, w.shape[1]).transpose(1, 0, 2)).reshape(128, -1)

    com["w_out"] = kmaj(inp["w_out"][0])
    com["wq"] = kmaj(inp["xattn_wq"][0])
    wkv = np.asarray(inp["xattn_wkv"][0], f)
    com["wkv"] = np.ascontiguousarray(wkv.reshape(16, 128, 16, 256).transpose(1, 2, 0, 3)).reshape(128, 16 * 16 * 256)
    com["wo"] = kmaj(inp["xattn_wo"][0])
    com["pwq"] = kmaj(inp["peer_wq"][0])
    sk = np.asarray(inp["peer_sub_keys"][0], f)
    com["skT"] = np.ascontiguousarray(sk.reshape(16, 128, 128).transpose(2, 0, 1)).reshape(128, 16 * 128)
    U = np.asarray(inp["peer_u"][0], f)
    com["UT"] = np.ascontiguousarray(U.reshape(128, 128, 16, 128).transpose(1, 3, 2, 0)).reshape(128 * 128, 2048)
    Vt = np.asarray(inp["peer_v"][0], f)
    com["Vp"] = np.ascontiguousarray(Vt.reshape(128, 128, 2048).transpose(1, 0, 2)).reshape(128 * 128, 2048)
    lnp = np.stack([np.asarray(inp[k][0], f) for k in ("ln1_g", "ln1_b", "ln2_g", "ln2_b", "ln3_g", "ln3_b")])
    com["lnp"] = np.ascontiguousarray(np.broadcast_to(lnp[:, None, :], (6, 128, 2048)))
    com["c_ident"] = np.eye(128, dtype=f)
    r = np.arange(128)
    com["c_triL"] = (r[:, None] <= r[None, :]).astype(f)
    com["c_triU"] = (r[:, None] > r[None, :]).astype(f)
    com["c_iota"] = np.ascontiguousarray(np.broadcast_to(r[None, :].astype(f), (128, 128)))
    maps = []
    x = np.asarray(inp["x"], f)
    mem = np.asarray(inp["mem"], f)
    for b in range(8):
        m = dict(com)
        m["xT"] = np.ascontiguousarray(x[b].T)
        m["xtok"] = np.ascontiguousarray(x[b])
        m["memT"] = np.ascontiguousarray(mem[b].T)
        maps.append(m)
    return maps


_NC_CACHE = {}


def kernel(**inputs):
    maps = _prep_inputs(inputs)
    if "nc" not in _NC_CACHE:
        _NC_CACHE["nc"] = build_program()
    nc = _NC_CACHE["nc"]
    res = run_bass_kernel_spmd(nc, maps, core_ids=list(range(8)))
    out = np.stack([np.asarray(r["out"], np.float32) for r in res.results], axis=0)
    return out
```
